# Optimizing a Trainium2 kernel written in Bass

```python
import math
import jax, jax.numpy as jnp
from jax import lax
import numpy as np

D_MODEL = 1024
BATCH = 2
SEQ = 8192
DEPTH = 2

N_MEM = 256
MEM_HEADS = 4
MEM_HEAD_DIM = 64
MEM_WIDTH = MEM_HEADS * MEM_HEAD_DIM
MIX_WIDTH = D_MODEL - MEM_WIDTH
CONV_WIDTH = 3
DIFF_HEAD_DIM = 64
DIFF_HEADS = MIX_WIDTH // (2 * DIFF_HEAD_DIM)
ROT_DIM = DIFF_HEAD_DIM // 4
ROPE_THETA = 500000.0
D_FF = ((8 * D_MODEL // 3 + 255) // 256) * 256
Q_BLOCK = 128
N_A = DEPTH // 2
N_B = DEPTH - N_A
EPS = 1e-6
MAX_POS_OFFSET = 4096

kernel_name = "yoco_shortconv_diffattn_macaron_memory"


def rmsnorm(x, g):
    xf = x.astype(jnp.float32)
    y = xf * lax.rsqrt(jnp.mean(xf * xf, axis=-1, keepdims=True) + EPS)
    return (y * g.astype(jnp.float32)).astype(x.dtype)


def swiglu(x, w1, w2):
    gate, up = jnp.split(x @ w1, 2, axis=-1)
    return (jax.nn.silu(gate) * up) @ w2


def rope_tables(positions):
    inv_freq = ROPE_THETA ** (-jnp.arange(0, ROT_DIM, 2, dtype=jnp.float32) / ROT_DIM)
    ang = positions.astype(jnp.float32)[..., None] * inv_freq
    return jnp.cos(ang), jnp.sin(ang)


def partial_rope(x, cos, sin):
    bshape = cos.shape[:2] + (1,) * (x.ndim - 3) + cos.shape[-1:]
    c = cos.reshape(bshape)
    s = sin.reshape(bshape)
    xf = x.astype(jnp.float32)
    x1 = xf[..., :ROT_DIM // 2]
    x2 = xf[..., ROT_DIM // 2:ROT_DIM]
    out = jnp.concatenate([x1 * c - x2 * s, x2 * c + x1 * s, xf[..., ROT_DIM:]], axis=-1)
    return out.astype(x.dtype)


def memory_kv(mem, g, w):
    kv = rmsnorm(mem, g) @ w
    k, v = jnp.split(kv, 2, axis=-1)
    shp = mem.shape[:2] + (MEM_HEADS, MEM_HEAD_DIM)
    return k.reshape(shp), v.reshape(shp)


def memory_attention(q, k, v):
    b, s, _ = q.shape
    qh = q.reshape(b, s, MEM_HEADS, MEM_HEAD_DIM)
    sc = jnp.einsum('bshd,bmhd->bhsm', qh, k, preferred_element_type=jnp.float32) * (MEM_HEAD_DIM ** -0.5)
    p = jax.nn.softmax(sc, axis=-1)
    o = jnp.einsum('bhsm,bmhd->bshd', p, v.astype(jnp.float32))
    return o.reshape(b, s, MEM_WIDTH).astype(q.dtype)


def causal_short_conv(x, w):
    return lax.conv_general_dilated(
        x, w[:, None, :].astype(x.dtype), window_strides=(1,),
        padding=[(CONV_WIDTH - 1, 0)], dimension_numbers=('NWC', 'WIO', 'NWC'),
        feature_group_count=x.shape[-1])


def differential_attention(q, k, v, lam):
    b, s = q.shape[:2]
    scale = DIFF_HEAD_DIM ** -0.5
    key_idx = jnp.arange(s)
    vf = v.astype(jnp.float32)

    def q_block(start):
        qb = lax.dynamic_slice_in_dim(q, start, Q_BLOCK, axis=1)
        sc = jnp.einsum('bqhcd,bkhcd->bhcqk', qb, k, preferred_element_type=jnp.float32) * scale
        causal = (start + jnp.arange(Q_BLOCK))[:, None] >= key_idx[None, :]
        p = jax.nn.softmax(jnp.where(causal, sc, -jnp.inf), axis=-1)
        a = p[:, :, 0] - lam * p[:, :, 1]
        return jnp.einsum('bhqk,bkhe->bqhe', a, vf)

    starts = jnp.arange(s // Q_BLOCK) * Q_BLOCK
    out = lax.map(q_block, starts)
    return jnp.moveaxis(out, 0, 1).reshape(b, s, DIFF_HEADS, 2 * DIFF_HEAD_DIM).astype(v.dtype)


def setup_inputs(seed: int = 0) -> dict:
    key = jax.random.key(seed)
    ks = jax.random.split(key, 24)
    f32 = jnp.float32

    def nrm(k, shape, fan_in):
        return jax.random.normal(k, shape, f32) * (fan_in ** -0.5)

    def gain(k, shape):
        return 1.0 + 0.02 * jax.random.normal(k, shape, f32)

    x = jax.random.normal(ks[0], (BATCH, SEQ, D_MODEL), f32)
    mem = jax.random.normal(ks[1], (BATCH, N_MEM, D_MODEL), f32)
    offsets = jax.random.randint(ks[2], (BATCH, 1), 0, MAX_POS_OFFSET, dtype=jnp.int32)
    positions = (offsets + jnp.arange(SEQ, dtype=jnp.int32)[None, :]).astype(jnp.int32)
    return {
        "x": x,
        "mem": mem,
        "positions": positions,
        "ffn_pre_norm": gain(ks[3], (DEPTH, D_MODEL)),
        "ffn_pre_w1": nrm(ks[4], (DEPTH, D_MODEL, 2 * D_FF), D_MODEL),
        "ffn_pre_w2": nrm(ks[5], (DEPTH, D_FF, D_MODEL), D_FF),
        "mix_norm": gain(ks[6], (DEPTH, D_MODEL)),
        "mem_norm": gain(ks[7], (DEPTH, D_MODEL)),
        "mem_w_kv": nrm(ks[8], (DEPTH, D_MODEL, 2 * MEM_WIDTH), D_MODEL),
        "a_w_in": nrm(ks[9], (N_A, D_MODEL, 3 * MIX_WIDTH + MEM_WIDTH), D_MODEL),
        "a_conv_w": nrm(ks[10], (N_A, CONV_WIDTH, MIX_WIDTH), CONV_WIDTH),
        "a_w_out": nrm(ks[11], (N_A, MIX_WIDTH + MEM_WIDTH, D_MODEL), MIX_WIDTH + MEM_WIDTH),
        "kv_norm": gain(ks[12], (D_MODEL,)),
        "kv_w": nrm(ks[13], (D_MODEL, 2 * MIX_WIDTH), D_MODEL),
        "b_w_q": nrm(ks[14], (N_B, D_MODEL, MIX_WIDTH + MEM_WIDTH), D_MODEL),
        "b_lambda": 0.1 * jax.random.normal(ks[15], (N_B, 4, DIFF_HEAD_DIM), f32),
        "b_subln": gain(ks[16], (N_B, 2 * DIFF_HEAD_DIM)),
        "b_w_out": nrm(ks[17], (N_B, MIX_WIDTH + MEM_WIDTH, D_MODEL), MIX_WIDTH + MEM_WIDTH),
        "ffn_post_norm": gain(ks[18], (DEPTH, D_MODEL)),
        "ffn_post_w1": nrm(ks[19], (DEPTH, D_MODEL, 2 * D_FF), D_MODEL),
        "ffn_post_w2": nrm(ks[20], (DEPTH, D_FF, D_MODEL), D_FF),
        "final_norm": gain(ks[21], (D_MODEL,)),
    }


def reference(x, mem, positions, ffn_pre_norm, ffn_pre_w1, ffn_pre_w2, mix_norm, mem_norm,
              mem_w_kv, a_w_in, a_conv_w, a_w_out, kv_norm, kv_w, b_w_q, b_lambda, b_subln,
              b_w_out, ffn_post_norm, ffn_post_w1, ffn_post_w2, final_norm):
    b, s, _ = x.shape
    cos, sin = rope_tables(positions)
    h = x
    k_sh = None
    v_sh = None
    for l in range(DEPTH):
        if l == N_A:
            kv = rmsnorm(h, kv_norm) @ kv_w
            k_sh = partial_rope(kv[..., :MIX_WIDTH].reshape(b, s, DIFF_HEADS, 2, DIFF_HEAD_DIM), cos, sin)
            v_sh = kv[..., MIX_WIDTH:].reshape(b, s, DIFF_HEADS, 2 * DIFF_HEAD_DIM)

        h = h + 0.5 * swiglu(rmsnorm(h, ffn_pre_norm[l]), ffn_pre_w1[l], ffn_pre_w2[l])

        mk, mv = memory_kv(mem, mem_norm[l], mem_w_kv[l])
        u = rmsnorm(h, mix_norm[l])
        if l < N_A:
            proj = u @ a_w_in[l]
            b_gate = proj[..., :MIX_WIDTH]
            c_gate = proj[..., MIX_WIDTH:2 * MIX_WIDTH]
            x_in = proj[..., 2 * MIX_WIDTH:3 * MIX_WIDTH]
            q_mem = proj[..., 3 * MIX_WIDTH:]
            y_main = b_gate * causal_short_conv(c_gate * x_in, a_conv_w[l])
            y_mem = memory_attention(q_mem, mk, mv)
            h = h + jnp.concatenate([y_main, y_mem], axis=-1) @ a_w_out[l]
        else:
            j = l - N_A
            lam_init = 0.8 - 0.6 * math.exp(-0.3 * l)
            proj = u @ b_w_q[j]
            q = partial_rope(proj[..., :MIX_WIDTH].reshape(b, s, DIFF_HEADS, 2, DIFF_HEAD_DIM), cos, sin)
            q_mem = proj[..., MIX_WIDTH:]
            lp = b_lambda[j].astype(jnp.float32)
            lam = jnp.exp(jnp.sum(lp[0] * lp[1])) - jnp.exp(jnp.sum(lp[2] * lp[3])) + lam_init
            o = differential_attention(q, k_sh, v_sh, lam)
            y_main = (rmsnorm(o, b_subln[j]) * (1.0 - lam_init)).reshape(b, s, MIX_WIDTH)
            y_mem = memory_attention(q_mem, mk, mv)
            h = h + jnp.concatenate([y_main, y_mem], axis=-1) @ b_w_out[j]

        h = h + 0.5 * swiglu(rmsnorm(h, ffn_post_norm[l]), ffn_post_w1[l], ffn_post_w2[l])
    return rmsnorm(h, final_norm)
```

```python
import math
from contextlib import ExitStack

import numpy as np
import ml_dtypes

import concourse.bass as bass
import concourse.mybir as mybir
from concourse.bass_utils import run_bass_kernel_spmd

F32 = mybir.dt.float32
BF16 = mybir.dt.bfloat16
I32 = mybir.dt.int32
AF = mybir.ActivationFunctionType
ALU = mybir.AluOpType

D = 1024
KT = 8
NTOK = 2048
NHALO = 32
NT = NTOK + NHALO
DFF = 2816
FT = 22
MIX = 768
EPS = 1e-6
LAM_INIT = 0.8 - 0.6 * math.exp(-0.3 * 1)
PI = math.pi
TILES_M = [(0, 0, 512), (1, 512, 512), (2, 1024, 512), (3, 1536, 512)]
TILES_H = [(4, 2048, 32)] + TILES_M
SKIP_FFN = False

G_PRE0, G_PRE1, G_MIX0, G_MIX1, G_MEM0, G_MEM1, G_KV, G_POST0, G_POST1, G_FIN = range(10)


class U:
    __slots__ = ("w", "r")

    def __init__(self):
        self.w = None
        self.r = {}


class Prog:
    NAMES = ["pe", "act", "dve", "pool", "sp"]

    def __init__(self, nc, es, nslots=20):
        self.nc = nc
        self.th = {e: [] for e in self.NAMES}
        self.sem = {e: es.enter_context(nc.semaphore("sem_" + e)) for e in self.NAMES}
        self.cnt = dict.fromkeys(self.NAMES, 0)
        self.seen = {e: {} for e in self.NAMES}
        self.dq = {}
        for q in ("sp", "pool"):
            self.dq[q] = dict(
                sems=[es.enter_context(nc.semaphore(f"dq_{q}{i}")) for i in range(nslots)],
                cnt=[0] * nslots,
                nxt=0,
            )

    def _wait(self, eng, ev):
        key, sem, val = ev
        if self.seen[eng].get(key, 0) >= val:
            return
        self.seen[eng][key] = val
        self.th[eng].append(lambda e, s=sem, v=val: e.wait_ge(s, v))

    def _deps(self, eng, reads, writes):
        evs = []
        for u in reads:
            if u.w is not None:
                evs.append(u.w)
        for u in writes:
            if u.w is not None:
                evs.append(u.w)
            evs.extend(u.r.values())
        for ev in evs:
            if eng == "pe" and ev[0] == "pe":
                continue
            self._wait(eng, ev)

    def _record(self, ev, reads, writes):
        for u in reads:
            old = u.r.get(ev[0])
            if old is None or old[2] < ev[2]:
                u.r[ev[0]] = ev
        for u in writes:
            u.w = ev
            u.r = {}

    def op(self, eng, method, kwargs, reads=(), writes=()):
        self._deps(eng, reads, writes)
        self.cnt[eng] += 1
        sem = self.sem[eng]
        self.th[eng].append(lambda e, m=method, k=kwargs, s=sem: getattr(e, m)(**k).then_inc(s, 1))
        ev = (eng, sem, self.cnt[eng])
        self._record(ev, reads, writes)
        return ev

    def mm(self, out_ap, pairs, reads=(), writes=(), start=True, stop=True):
        self._deps("pe", reads, writes)
        self.cnt["pe"] += 1
        sem = self.sem["pe"]
        n = len(pairs)

        def fn(pe, out_ap=out_ap, pairs=pairs, n=n, start=start, stop=stop, sem=sem):
            ins = None
            for idx, (l, r) in enumerate(pairs):
                ins = pe.matmul(out_ap, l, r, start=(start and idx == 0), stop=(stop and idx == n - 1))
            ins.then_inc(sem, 1)

        self.th["pe"].append(fn)
        ev = ("pe", sem, self.cnt["pe"])
        self._record(ev, reads, writes)
        return ev

    def dma(self, q, out, in_, reads=(), writes=(), **kw):
        d = self.dq[q]
        i = d["nxt"]
        d["nxt"] = (i + 1) % len(d["sems"])
        sem = d["sems"][i]
        key = ("dq", q, i)
        if d["cnt"][i] > 0:
            self._wait(q, (key, sem, d["cnt"][i]))
        self._deps(q, reads, writes)
        d["cnt"][i] += 16
        self.th[q].append(lambda e, o=out, a=in_, s=sem, k=kw: e.dma_start(out=o, in_=a, **k).then_inc(s, 16))
        ev = (key, sem, d["cnt"][i])
        self._record(ev, reads, writes)
        return ev

    def all_events(self):
        evs = [(e, self.sem[e], self.cnt[e]) for e in self.NAMES if self.cnt[e] > 0]
        for q, d in self.dq.items():
            for i, s in enumerate(d["sems"]):
                if d["cnt"][i] > 0:
                    evs.append((("dq", q, i), s, d["cnt"][i]))
        return evs

    def barrier(self, engines=None):
        evs = self.all_events()
        for e in engines or self.NAMES:
            for ev in evs:
                if ev[0] == e:
                    continue
                self._wait(e, ev)

    def wait_event(self, eng, ev):
        self._wait(eng, ev)

    def finish(self):
        for q, d in self.dq.items():
            for i, s in enumerate(d["sems"]):
                if d["cnt"][i] > 0:
                    self._wait(q, (("dq", q, i), s, d["cnt"][i]))
        self.barrier()

    def emit(self, block):
        nc = self.nc

        def run(name):
            def f(eng):
                for t in self.th[name]:
                    t(eng)
            return f

        block.tensor(run("pe"))
        block.scalar(run("act"))
        block.vector(run("dve"))
        block.gpsimd(run("pool"))
        block.sync(run("sp"))


class TT:
    def __init__(self, ap):
        self.ap = ap
        self.units = {}

    def u(self, key=0):
        x = self.units.get(key)
        if x is None:
            x = self.units[key] = U()
        return x


def build(mode, stop_after=None, seg=None):
    nc = bass.Bass("TRN2", target_bir_lowering=False)
    es = ExitStack()
    doA = mode in ("A", "F")
    doB = mode in ("B", "F")

    used_inputs = []

    class LZ:
        def __init__(self, name, shape, dt):
            self.name, self.shape, self.dt, self._ap = name, list(shape), dt, None

        def get(self):
            if self._ap is None:
                self._ap = nc.dram_tensor(self.name, self.shape, self.dt, kind="ExternalInput").ap()
                used_inputs.append(self.name)
            return self._ap

        def __getitem__(self, idx):
            return self.get()[idx]

        def rearrange(self, *a, **k):
            return self.get().rearrange(*a, **k)

    def din(name, shape, dt):
        return LZ(name, shape, dt)

    def dout(name, shape, dt):
        return nc.dram_tensor(name, list(shape), dt, kind="ExternalOutput").ap()

    gains_d = din("gains", [128, 10 * KT], F32)
    xT_d = din("xT", [128, KT, NT], F32)
    memT_d = din("memT", [128, KT, 256], F32)
    pos_d = din("pos", [128, NTOK], I32)
    convw_d = din("convw", [128, 6 * 3], F32)
    invf_d = din("invf", [128, 1], F32)
    rot_d = din("rotm", [128, 128], BF16)
    w1_d = {n: din("w1_" + n, [44, 128, KT, 128], F32) for n in ("pre0", "post0", "pre1")}
    w2_d = {n: din("w2_" + n, [8, 128, FT, 128], F32) for n in ("pre0", "post0", "pre1")}
    memw_d = [din(f"memw{l}", [4, 128, KT, 128], F32) for l in range(2)]
    awin_d = din("a_w_in", [20, 128, KT, 128], F32)
    awout_d = din("a_w_out", [8, 128, KT, 128], F32)
    kvw_d = din("kv_w", [12, 128, KT, 128], F32)
    bwq_d = din("b_w_q", [8, 128, KT, 128], F32)
    mask_d = din("masks", [128, 4 * 128], BF16)
    lam_d = din("lam", [128, 256], F32)
    subln_d = din("subln", [128, 1], F32)
    bwout_d = din("b_w_out", [8, 128, KT, 128], F32)
    w1_post1 = din("w1_post1", [44, 128, KT, 128], F32)
    w2_post1 = din("w2_post1", [8, 128, FT, 128], F32)
    h_i = din("h_in", [128, KT, NTOK], F32)
    y_i = din("y_in", [128, KT, NTOK], BF16)
    q_i = din("q_in", [128, 6, NTOK], BF16)
    ym_i = din("ym_in", [128, 2, NTOK], BF16)
    if mode == "F":
        kloc_d = nc.dram_tensor("k_loc", [128, 6 * NTOK], BF16).ap()
        vloc_d = nc.dram_tensor("v_loc", [128, 6 * NTOK], BF16).ap()
        kg_d = nc.dram_tensor("k_g", [512, 6 * NTOK], BF16).ap()
        vg_d = nc.dram_tensor("v_g", [512, 6 * NTOK], BF16).ap()
    else:
        kg_d = din("k_g", [512, 6 * NTOK], BF16)
        vg_d = din("v_g", [512, 6 * NTOK], BF16)
    if mode == "A":
        h_o = dout("h_io", [128, KT, NTOK], F32)
        if seg in (None, "A2"):
            kloc_d = dout("k_loc", [128, 6 * NTOK], BF16)
            vloc_d = dout("v_loc", [128, 6 * NTOK], BF16)
        if seg in (None, "A3"):
            q_o = dout("q_io", [128, 6, NTOK], BF16)
            ym_o = dout("ym_io", [128, 2, NTOK], BF16)
    if mode == "B":
        if seg == "B1":
            y_o = dout("y_io", [128, KT, NTOK], BF16)
        else:
            outT_d = dout("outT", [128, KT, NTOK], F32)
    if mode == "F":
        outT_d = dout("outT", [128, KT, NTOK], F32)

    def sb(name, shape, dt):
        return es.enter_context(nc.sbuf_tensor(name, list(shape), dt))

    hT = TT(sb("hT", [128, KT, NT], F32)[:])
    uT = TT(sb("uT", [128, KT, NT], BF16)[:])
    gains = TT(sb("gains_sb", [128, 10 * KT], F32)[:])
    ones = TT(sb("ones_sb", [128, 128], BF16)[:])
    NW = 6
    WSLOT = 1408
    wring_t = sb("wring", [128, NW, WSLOT], BF16)
    wring = [TT(wring_t[:, i, :]) for i in range(NW)]
    XW = 92160 // 4
    xreg = sb("xreg", [128, XW], F32)
    psb = [TT(es.enter_context(nc.psum_tensor(f"ps{i}", [128, 512], F32))[:]) for i in range(8)]

    P = Prog(nc, es)
    st = dict(w=0, ps=0, xoff=0)

    def xalloc(shape, dt):
        n = int(np.prod(shape[1:]))
        nbytes = n * (4 if dt in (F32, I32) else 2)
        nbytes = (nbytes + 31) // 32 * 32
        off = st["xoff"]
        assert off + nbytes <= XW * 4, f"xreg overflow {off + nbytes}"
        st["xoff"] = off + nbytes
        ap = xreg[:, off // 4:(off + nbytes) // 4]
        if dt != F32:
            ap = ap.bitcast(dt)
        ap = ap[:, 0:n]
        if len(shape) == 3:
            ap = ap.rearrange("p (a b) -> p a b", b=shape[2])
        elif len(shape) == 4:
            ap = ap.rearrange("p (a b c) -> p a b c", b=shape[2], c=shape[3])
        return TT(ap)

    def phase():
        P.barrier()
        st["xoff"] = 0

    def next_ps(ring=None):
        ring = ring or range(8)
        i = st["ps"]
        st["ps"] = i + 1
        return psb[ring[i % len(ring)]]

    def load_w(dram_mtile, nelem):
        i = st["w"]
        st["w"] = i + 1
        slot = wring[i % NW]
        P.dma("pool", slot.ap[:, 0:nelem], dram_mtile, writes=[slot.u()])
        return slot

    def wflat(d):
        return d.rearrange("p k c -> p (k c)")

    def mm(out_u, out_ap, pairs, reads):
        return P.mm(out_ap, pairs, reads=reads, writes=[out_u])

    def ACT(out, in_, func, reads, writes, **kw):
        P.op("act", "activation", dict(out=out, in_=in_, func=func, **kw), reads, writes)

    def TTO(out, in0, in1, op, reads, writes, eng="dve"):
        P.op(eng, "tensor_tensor", dict(out=out, in0=in0, in1=in1, op=op), reads, writes)

    def STT(out, in0, scalar, in1, op0, op1, reads, writes):
        P.op("dve", "scalar_tensor_tensor", dict(out=out, in0=in0, scalar=scalar, in1=in1, op0=op0, op1=op1), reads, writes)

    def TS(out, in0, s1, s2, op0, op1, reads, writes):
        kw = dict(out=out, in0=in0, scalar1=s1, scalar2=s2, op0=op0)
        if op1 is not None:
            kw["op1"] = op1
        P.op("dve", "tensor_scalar", kw, reads, writes)

    def gcol(g, k):
        return gains.ap[:, g * KT + k:g * KT + k + 1]

    def proj_mm(ps, w, wslot, src, tt, c0, src_k=KT):
        mm(ps.u(), ps.ap[:, 0:w], [(wslot.ap[:, k * 128:(k + 1) * 128], src.ap[:, k, c0:c0 + w]) for k in range(src_k)],
           [src.u((k, tt)) for k in range(src_k)] + [wslot.u()])

    P.dma("sp", gains.ap, gains_d.get(), writes=[gains.u()])
    P.op("dve", "memset", dict(ap=ones.ap, constant=1.0), [], [ones.u()])

    def rmsnorm(src, dst, g, tiles, nk=KT, mean_n=D, gain_col=None, dkey=None):
        sq = xalloc([128, 16, 512], BF16)
        lnv = xalloc([128, 2, 512], F32)
        epsc = xalloc([128, 1], F32)
        P.op("dve", "memset", dict(ap=epsc.ap, constant=EPS), [], [epsc.u()])
        for ti, (tt, c0, w) in enumerate(tiles):
            ss = [(ti * nk + k) % 16 for k in range(nk)]
            for k in range(nk):
                ACT(sq.ap[:, ss[k], 0:w], src.ap[:, k, c0:c0 + w], AF.Square, [src.u((k, tt))], [sq.u(ss[k])])
            ps = next_ps()
            mm(ps.u(), ps.ap[:, 0:w], [(ones.ap, sq.ap[:, s, 0:w]) for s in ss], [ones.u()] + [sq.u(s) for s in ss])
            l = ti % 2
            ACT(lnv.ap[:, l, 0:w], ps.ap[:, 0:w], AF.Ln, [ps.u(), epsc.u()], [lnv.u(l)], bias=epsc.ap, scale=1.0 / mean_n)
            ps2 = next_ps()
            ACT(ps2.ap[:, 0:w], lnv.ap[:, l, 0:w], AF.Exp, [lnv.u(l)], [ps2.u()], scale=-0.5)
            for k in range(nk):
                gc = gcol(g, k) if gain_col is None else gain_col
                STT(dst.ap[:, k, c0:c0 + w], src.ap[:, k, c0:c0 + w], gc, ps2.ap[:, 0:w], ALU.mult, ALU.mult,
                    [src.u((k, tt)), ps2.u(), gains.u()], [dst.u((k, tt))])

    def ffn(w1d, w2d, g, tiles):
        phase()
        rmsnorm(hT, uT, g, tiles)
        hid = xalloc([128, 11, NT], BF16)
        sg = xalloc([128, 3, 512], BF16)
        nsg = 0
        for half in range(2):
            for f in range(11):
                fa = half * 11 + f
                wg = load_w(wflat(w1d[fa]), KT * 128)
                wu = load_w(wflat(w1d[FT + fa]), KT * 128)
                for (tt, c0, w) in tiles:
                    psg = next_ps()
                    psu = next_ps()
                    proj_mm(psg, w, wg, uT, tt, c0)
                    proj_mm(psu, w, wu, uT, tt, c0)
                    s = nsg % 3
                    nsg += 1
                    ACT(sg.ap[:, s, 0:w], psg.ap[:, 0:w], AF.Silu, [psg.u()], [sg.u(s)])
                    TTO(hid.ap[:, f, c0:c0 + w], psu.ap[:, 0:w], sg.ap[:, s, 0:w], ALU.mult, [psu.u(), sg.u(s)], [hid.u((f, tt))])
            for m in range(8):
                w2s = load_w(w2d[m][:, half * 11:(half + 1) * 11, :].rearrange("p f c -> p (f c)"), 11 * 128)
                for (tt, c0, w) in tiles:
                    ps = next_ps()
                    mm(ps.u(), ps.ap[:, 0:w], [(w2s.ap[:, f * 128:(f + 1) * 128], hid.ap[:, f, c0:c0 + w]) for f in range(11)],
                       [hid.u((f, tt)) for f in range(11)] + [w2s.u()])
                    STT(hT.ap[:, m, c0:c0 + w], ps.ap[:, 0:w], 0.5, hT.ap[:, m, c0:c0 + w], ALU.mult, ALU.add,
                        [ps.u(), hT.u((m, tt))], [hT.u((m, tt))])

    def outproj(wd, yT):
        for m in range(8):
            ws = load_w(wflat(wd[m]), KT * 128)
            for (tt, c0, w) in TILES_M:
                ps = next_ps()
                proj_mm(ps, w, ws, yT, tt, c0)
                TTO(hT.ap[:, m, c0:c0 + w], ps.ap[:, 0:w], hT.ap[:, m, c0:c0 + w], ALU.add, [ps.u(), hT.u((m, tt))], [hT.u((m, tt))])

    def mem_kv(l, mkT, mv):
        memT = xalloc([128, KT, 256], F32)
        un = xalloc([128, KT, 256], BF16)
        mw = xalloc([128, 4, KT * 128], BF16)
        P.dma("sp", memT.ap, memT_d.get(), writes=[memT.u((k, 0)) for k in range(KT)])
        for m in range(4):
            P.dma("pool", mw.ap[:, m, :], wflat(memw_d[l][m]), writes=[mw.u(m)])
        rmsnorm(memT, un, G_MEM0 + l, [(0, 0, 256)])
        for pair in range(2):
            ps = next_ps()
            mm(ps.u(), ps.ap[:, 0:256], [(mw.ap[:, pair, k * 128:(k + 1) * 128], un.ap[:, k, :]) for k in range(KT)],
               [un.u((k, 0)) for k in range(KT)] + [mw.u(pair)])
            ACT(mkT.ap[:, pair, :], ps.ap[:, 0:256], AF.Copy, [ps.u()], [mkT.u(pair)])
        mwv = mw.ap.rearrange("p m (k c) -> p m k c", c=128)
        for mt in range(2):
            ps = next_ps()
            mm(ps.u(), ps.ap[:, 0:256].rearrange("p (a b) -> p a b", b=128),
               [(un.ap[:, k, mt * 128:(mt + 1) * 128], mwv[:, 2:4, k, :]) for k in range(KT)],
               [un.u((k, 0)) for k in range(KT)] + [mw.u(2), mw.u(3)])
            ACT(mv.ap[:, mt, :], ps.ap[:, 0:256], AF.Copy, [ps.u()], [mv.u(mt)])

    def mem_attn(yap, yunit, mkT, mv):
        pT = xalloc([128, 4, 512], BF16)
        rc = xalloc([128, 2, 512], F32)
        npt = 0
        nrc = 0
        for (tt, c0, w) in TILES_M:
            for pair in range(2):
                pso = next_ps()
                pss = next_ps()
                for hh in range(2):
                    p0 = hh * 64
                    pts = []
                    for mt in range(2):
                        ps = next_ps()
                        mm(ps.u(), ps.ap[:, 0:w], [(mkT.ap[p0:p0 + 64, pair, mt * 128:(mt + 1) * 128], yap[p0:p0 + 64, pair, c0:c0 + w])],
                           [mkT.u(pair), yunit(pair, tt)])
                        s = npt % 4
                        npt += 1
                        ACT(pT.ap[:, s, 0:w], ps.ap[:, 0:w], AF.Exp, [ps.u()], [pT.u(s)], scale=0.125)
                        pts.append(s)
                    hcol = (pair * 2 + hh) * 64
                    mm(pso.u(hh), pso.ap[p0:p0 + 64, 0:w], [(mv.ap[:, mt, hcol:hcol + 64], pT.ap[:, pts[mt], 0:w]) for mt in range(2)],
                       [mv.u(0), mv.u(1), pT.u(pts[0]), pT.u(pts[1])])
                    mm(pss.u(hh), pss.ap[p0:p0 + 64, 0:w], [(ones.ap[:, 0:64], pT.ap[:, pts[mt], 0:w]) for mt in range(2)],
                       [ones.u(), pT.u(pts[0]), pT.u(pts[1])])
                r = nrc % 2
                nrc += 1
                P.op("dve", "reciprocal", dict(out=rc.ap[:, r, 0:w], in_=pss.ap[:, 0:w]), [pss.u(0), pss.u(1)], [rc.u(r)])
                TTO(yap[:, pair, c0:c0 + w], pso.ap[:, 0:w], rc.ap[:, r, 0:w], ALU.mult, [pso.u(0), pso.u(1), rc.u(r)], [yunit(pair, tt)])

    def rope_tables():
        Ct = xalloc([128, NTOK], F32)
        St = xalloc([128, NTOK], F32)
        invf = xalloc([128, 1], F32)
        mark = st["xoff"]
        posi = xalloc([128, NTOK], I32)
        ang = xalloc([128, NTOK], F32)
        kf = xalloc([128, NTOK], F32)
        P.dma("sp", posi.ap, pos_d.get(), writes=[posi.u()])
        P.dma("sp", invf.ap, invf_d.get(), writes=[invf.u()])
        P.op("dve", "tensor_copy", dict(out=ang.ap, in_=posi.ap), [posi.u()], [ang.u()])
        TS(ang.ap, ang.ap, invf.ap, None, ALU.mult, None, [ang.u(), invf.u()], [ang.u()])
        TS(posi.ap, ang.ap, 1.0 / (2 * PI), None, ALU.mult, None, [ang.u()], [posi.u()])
        P.op("dve", "tensor_copy", dict(out=kf.ap, in_=posi.ap), [posi.u()], [kf.u()])
        C1 = 6.28125
        C2 = 2 * PI - C1
        for cc in (C1, C2):
            STT(ang.ap, kf.ap, -cc, ang.ap, ALU.mult, ALU.add, [kf.u(), ang.u()], [ang.u()])
        for dst, shift in ((St, 0.0), (Ct, PI / 2)):
            TS(kf.ap, ang.ap, shift, PI, ALU.add, ALU.is_gt, [ang.u()], [kf.u()])
            STT(dst.ap, kf.ap, -2 * PI, ang.ap, ALU.mult, ALU.add, [kf.u(), ang.u()], [dst.u()])
            TS(dst.ap, dst.ap, shift, -PI, ALU.add, ALU.max, [dst.u()], [dst.u()])
            TS(dst.ap, dst.ap, PI, None, ALU.min, None, [dst.u()], [dst.u()])
            ACT(dst.ap, dst.ap, AF.Sin, [dst.u()], [dst.u()])
        P.barrier()
        st["xoff"] = mark
        return Ct, St

    def rope_evac(ps, Ct, St, rotm, ksb, t12, dst_ap, c0, w, dst_units, n):
        s = n % 2
        P.op("dve", "tensor_copy", dict(out=ksb.ap[:, s, 0:w], in_=ps.ap[:, 0:w]), [ps.u()], [ksb.u(s)])
        ps2 = next_ps()
        mm(ps2.u(), ps2.ap[:, 0:w], [(rotm.ap, ksb.ap[:, s, 0:w])], [rotm.u(), ksb.u(s)])
        TTO(t12.ap[:, 2 * s, 0:w], ps.ap[:, 0:w], Ct.ap[:, c0:c0 + w], ALU.mult, [ps.u(), Ct.u()], [t12.u(2 * s)])
        TTO(t12.ap[:, 2 * s + 1, 0:w], ps2.ap[:, 0:w], St.ap[:, c0:c0 + w], ALU.mult, [ps2.u(), St.u()], [t12.u(2 * s + 1)])
        TTO(dst_ap, t12.ap[:, 2 * s, 0:w], t12.ap[:, 2 * s + 1, 0:w], ALU.add, [t12.u(2 * s), t12.u(2 * s + 1)], dst_units)

    def load_h():
        for k in range(KT):
            P.dma("sp", hT.ap[:, k, 0:NTOK], h_i[:, k, :], writes=[hT.u((k, tt)) for tt in range(4)])

    def part_A():
        if seg in (None, "A1"):
            r = part_A1()
            if seg == "A1" or stop_after is not None:
                return r
        if seg == "A2":
            load_h()
        if seg in (None, "A2"):
            r = part_A2()
            if seg == "A2" or stop_after is not None:
                return r
        if seg == "A3":
            load_h()
        return part_A3()

    def part_A1():
        for k in range(KT):
            P.dma("sp", hT.ap[:, k, :], xT_d[:, k, :], writes=[hT.u((k, tt)) for tt in range(5)])
        if not SKIP_FFN:
            ffn(w1_d["pre0"], w2_d["pre0"], G_PRE0, TILES_H)
        if stop_after == "ffn0":
            return None
        phase()
        mkT = xalloc([128, 2, 256], BF16)
        mv = xalloc([128, 2, 256], BF16)
        convw = xalloc([128, 18], F32)
        P.dma("sp", convw.ap, convw_d.get(), writes=[convw.u()])
        keep = st["xoff"]
        mem_kv(0, mkT, mv)
        P.barrier()
        if stop_after == "memkv":
            return None
        st["xoff"] = keep
        rmsnorm(hT, uT, G_MIX0, TILES_H)
        P.barrier()
        st["xoff"] = keep
        yT = xalloc([128, KT, NTOK], BF16)
        zb = xalloc([128, 2, 16, 130], F32)
        csb = xalloc([128, 2, 512], F32)
        acc = xalloc([128, 2, 512], F32)
        nz = 0
        for i in range(6):
            wb = load_w(wflat(awin_d[i]), KT * 128)
            wc = load_w(wflat(awin_d[6 + i]), KT * 128)
            wx = load_w(wflat(awin_d[12 + i]), KT * 128)
            zi = i % 2
            for (tt, c0, w) in TILES_H:
                psc = next_ps()
                psx = next_ps()
                proj_mm(psc, w, wc, uT, tt, c0)
                proj_mm(psx, w, wx, uT, tt, c0)
                s = nz % 2
                nz += 1
                ACT(csb.ap[:, s, 0:w], psc.ap[:, 0:w], AF.Copy, [psc.u()], [csb.u(s)])
                if tt == 4:
                    TTO(zb.ap[:, zi, :, 0:2], psx.ap[:, 0:32].rearrange("p (a b) -> p a b", b=2),
                        csb.ap[:, s, 0:32].rearrange("p (a b) -> p a b", b=2), ALU.mult,
                        [psx.u(), csb.u(s)], [zb.u((zi, t, "h")) for t in range(4)])
                    continue
                psbg = next_ps()
                proj_mm(psbg, w, wb, uT, tt, c0)
                zu = zb.u((zi, tt))
                zh = zb.u((zi, tt, "h"))
                zv = zb.ap[:, zi, 4 * tt:4 * tt + 4, :]
                TTO(zv[:, :, 2:130], psx.ap[:, 0:512].rearrange("p (a b) -> p a b", b=128),
                    csb.ap[:, s, 0:512].rearrange("p (a b) -> p a b", b=128), ALU.mult, [psx.u(), csb.u(s)], [zu])
                a3 = acc.ap[:, s, :].rearrange("p (a b) -> p a b", b=128)
                TS(a3, zv[:, :, 0:128], convw.ap[:, i * 3:i * 3 + 1], None, ALU.mult, None, [zu, zh, convw.u()], [acc.u(s)])
                for j in (1, 2):
                    STT(a3, zv[:, :, j:j + 128], convw.ap[:, i * 3 + j:i * 3 + j + 1], a3, ALU.mult, ALU.add,
                        [zu, zh, convw.u(), acc.u(s)], [acc.u(s)])
                TTO(yT.ap[:, i, c0:c0 + 512], psbg.ap[:, 0:512], acc.ap[:, s, :], ALU.mult, [psbg.u(), acc.u(s)], [yT.u((i, tt))])
        if stop_after == "conv":
            return None
        for pair in range(2):
            wq = load_w(wflat(awin_d[18 + pair]), KT * 128)
            for (tt, c0, w) in TILES_M:
                ps = next_ps()
                proj_mm(ps, w, wq, uT, tt, c0)
                ACT(yT.ap[:, 6 + pair, c0:c0 + w], ps.ap[:, 0:w], AF.Copy, [ps.u()], [yT.u((6 + pair, tt))])
        if stop_after == "qmem":
            return None
        mem_attn(yT.ap[:, 6:8, :], lambda pair, tt: yT.u((6 + pair, tt)), mkT, mv)
        if stop_after == "ymix0":
            return ("y", yT)
        outproj(awout_d, yT)
        return None

    def part_A2():
        if not SKIP_FFN:
            ffn(w1_d["post0"], w2_d["post0"], G_POST0, TILES_M)
        if stop_after == "post0":
            return None
        phase()
        rmsnorm(hT, uT, G_KV, TILES_M)
        P.barrier()
        st["xoff"] = 0
        if stop_after == "kvnorm":
            return None
        Ct, St = rope_tables()
        if stop_after == "tables":
            return None
        rotm = xalloc([128, 128], BF16)
        P.dma("sp", rotm.ap, rot_d.get(), writes=[rotm.u()])
        ksb = xalloc([128, 2, 512], BF16)
        t12 = xalloc([128, 4, 512], F32)
        kst = xalloc([128, 2, NTOK], BF16)
        vw = xalloc([128, 6, KT * 128], BF16)
        vst = xalloc([128, 3, 768], BF16)
        nr = 0
        for hd in range(6):
            wk = load_w(wflat(kvw_d[hd]), KT * 128)
            ks = hd % 2
            for (tt, c0, w) in TILES_M:
                ps = next_ps()
                proj_mm(ps, w, wk, uT, tt, c0)
                rope_evac(ps, Ct, St, rotm, ksb, t12, kst.ap[:, ks, c0:c0 + w], c0, w, [kst.u((ks, tt))], nr)
                nr += 1
            P.dma("sp", kloc_d[:, hd * NTOK:(hd + 1) * NTOK], kst.ap[:, ks, :], reads=[kst.u((ks, tt)) for tt in range(4)])
        if stop_after == "kproj":
            return None
        for m in range(6):
            P.dma("pool", vw.ap[:, m, :], wflat(kvw_d[6 + m]), writes=[vw.u(m)])
        vwv = vw.ap.rearrange("p m (k c) -> p m k c", c=128)
        vloc_v = vloc_d.rearrange("p (h l e) -> p h l e", h=6, l=16)
        for lb in range(16):
            tt = lb // 4
            vs = lb % 3
            for half in range(2):
                ps = next_ps()
                mm(ps.u(), ps.ap[:, 0:384].rearrange("p (a b) -> p a b", b=128),
                   [(uT.ap[:, k, lb * 128:(lb + 1) * 128], vwv[:, 3 * half:3 * half + 3, k, :]) for k in range(KT)],
                   [uT.u((k, tt)) for k in range(KT)] + [vw.u(3 * half + x) for x in range(3)])
                ACT(vst.ap[:, vs, half * 384:(half + 1) * 384], ps.ap[:, 0:384], AF.Copy, [ps.u()], [vst.u((vs, half))])
            P.dma("sp", vloc_v[:, :, lb, :], vst.ap[:, vs, :].rearrange("p (h e) -> p h e", e=128),
                  reads=[vst.u((vs, 0)), vst.u((vs, 1))])
        return None

    def part_A3():
        if not SKIP_FFN:
            ffn(w1_d["pre1"], w2_d["pre1"], G_PRE1, TILES_M)
        phase()
        qT = xalloc([128, 6, NTOK], BF16)
        yq = xalloc([128, 2, NTOK], BF16)
        mkT = xalloc([128, 2, 256], BF16)
        mv = xalloc([128, 2, 256], BF16)
        keep = st["xoff"]
        mem_kv(1, mkT, mv)
        P.barrier()
        st["xoff"] = keep
        rmsnorm(hT, uT, G_MIX1, TILES_M)
        P.barrier()
        st["xoff"] = keep
        Ct, St = rope_tables()
        rotm = xalloc([128, 128], BF16)
        P.dma("sp", rotm.ap, rot_d.get(), writes=[rotm.u()])
        ksb = xalloc([128, 2, 512], BF16)
        t12 = xalloc([128, 4, 512], F32)
        nr = 0
        for hd in range(6):
            wq = load_w(wflat(bwq_d[hd]), KT * 128)
            for (tt, c0, w) in TILES_M:
                ps = next_ps()
                proj_mm(ps, w, wq, uT, tt, c0)
                rope_evac(ps, Ct, St, rotm, ksb, t12, qT.ap[:, hd, c0:c0 + w], c0, w, [qT.u((hd, tt))], nr)
                nr += 1
        for pair in range(2):
            wq = load_w(wflat(bwq_d[6 + pair]), KT * 128)
            for (tt, c0, w) in TILES_M:
                ps = next_ps()
                proj_mm(ps, w, wq, uT, tt, c0)
                ACT(yq.ap[:, pair, c0:c0 + w], ps.ap[:, 0:w], AF.Copy, [ps.u()], [yq.u((pair, tt))])
        mem_attn(yq.ap, lambda pair, tt: yq.u((pair, tt)), mkT, mv)
        return ("q", qT, yq)

    def part_B(qT, yq):
        yT = uT
        masks = xalloc([128, 4, 128], BF16)
        lam = xalloc([128, 256], F32)
        lw = xalloc([128, 8], F32)
        gsub = xalloc([128, 1], F32)
        epsc = xalloc([128, 1], F32)
        P.dma("sp", masks.ap.rearrange("p a b -> p (a b)"), mask_d.get(), writes=[masks.u()])
        P.dma("sp", lam.ap, lam_d.get(), writes=[lam.u()])
        P.dma("sp", gsub.ap, subln_d.get(), writes=[gsub.u()])
        P.op("dve", "memset", dict(ap=epsc.ap, constant=EPS), [], [epsc.u()])
        for pair in range(2):
            for (tt, c0, w) in TILES_M:
                P.op("dve", "tensor_copy", dict(out=yT.ap[:, 6 + pair, c0:c0 + w], in_=yq.ap[:, pair, c0:c0 + w]),
                     [yq.u((pair, tt))], [yT.u((6 + pair, tt))])
        for i in range(2):
            P.op("dve", "tensor_tensor", dict(out=lam.ap[:, 128 * i:128 * i + 64], in0=lam.ap[:, 128 * i:128 * i + 64],
                                              in1=lam.ap[:, 128 * i + 64:128 * i + 128], op=ALU.mult), [lam.u()], [lam.u()])
            P.op("dve", "reduce_sum", dict(out=lw.ap[:, i:i + 1], in_=lam.ap[:, 128 * i:128 * i + 64], axis=mybir.AxisListType.X),
                 [lam.u()], [lw.u()])
        ACT(lw.ap[:, 2:4], lw.ap[:, 0:2], AF.Exp, [lw.u()], [lw.u()])
        TTO(lw.ap[:, 4:5], lw.ap[:, 3:4], lw.ap[:, 2:3], ALU.subtract, [lw.u()], [lw.u()])
        TS(lw.ap[:, 5:6], lw.ap[:, 4:5], -LAM_INIT, None, ALU.add, None, [lw.u()], [lw.u()])
        neglam = lw.ap[:, 5:6]
        TS(gsub.ap, gsub.ap, 1.0 - LAM_INIT, None, ALU.mult, None, [gsub.u()], [gsub.u()])

        NCH = 4
        kch = xalloc([128, NCH, 2048], BF16)
        vch = xalloc([128, NCH, 16, 128], BF16)
        pT = xalloc([128, 6, 512], BF16)
        fin = xalloc([128, 5, 512], F32)
        osq = xalloc([128, 512], BF16)
        lnv = xalloc([128, 512], F32)
        kg_v = kg_d.rearrange("(r p) (h t) -> p r h t", p=128, h=6)
        vg_v = vg_d.rearrange("(r p) (h t) -> p r h t", p=128, h=6)
        S_RING = [0, 1, 2, 3]
        npt = 0
        nps = [0]

        def s_ps():
            i = nps[0]
            nps[0] += 1
            return psb[S_RING[i % 4]]

        jobs = [(hd, qi, ci) for hd in range(6) for qi in range(4) for ci in range(qi + 1)]

        def load_chunk(n):
            hd, qi, ci = jobs[n]
            slot = n % NCH
            P.dma("sp", kch.ap[:, slot, :].rearrange("p (r t) -> p r t", r=4), kg_v[:, :, hd, ci * 512:(ci + 1) * 512],
                  writes=[kch.u(slot)])
            P.dma("sp", vch.ap[:, slot, :, :].rearrange("p (r l) e -> p r (l e)", r=4), vg_v[:, :, hd, ci * 512:(ci + 1) * 512],
                  writes=[vch.u(slot)])

        load_chunk(0)
        load_chunk(1)
        jn = 0
        for hd in range(6):
            for qi in range(4):
                q0 = qi * 512
                po = [psb[4], psb[5]]
                pz = [psb[6], psb[7]]
                pend = None

                def do_pv(pend, first, last):
                    slot, bi, lo, pts = pend
                    for c in range(2):
                        P.mm(po[c].ap[:, lo:512], [(vch.ap[:, slot, bi, :], pT.ap[:, pts[c], lo:512])],
                             reads=[vch.u(slot), pT.u(pts[c])], writes=[po[c].u()], start=first, stop=last)
                        P.mm(pz[c].ap[:, lo:512], [(ones.ap, pT.ap[:, pts[c], lo:512])],
                             reads=[ones.u(), pT.u(pts[c])], writes=[pz[c].u()], start=first, stop=last)

                it = 0
                for ci in range(qi + 1):
                    assert jobs[jn] == (hd, qi, ci)
                    slot = jn % NCH
                    if jn + 2 < len(jobs):
                        load_chunk(jn + 2)
                    jn += 1
                    for jp in range(4):
                        for lbl in range(4):
                            bi = jp * 4 + lbl
                            lo, mj = (0, None) if ci < qi else (128 * lbl, jp)
                            pts = []
                            for c in range(2):
                                ps = s_ps()
                                p0 = 64 * c
                                P.mm(ps.ap[:, lo:512], [(kch.ap[p0:p0 + 64, slot, bi * 128:(bi + 1) * 128], qT.ap[p0:p0 + 64, hd, q0 + lo:q0 + 512])],
                                     reads=[kch.u(slot), qT.u((hd, qi))], writes=[ps.u()])
                                s_ = npt % 6
                                npt += 1
                                ACT(pT.ap[:, s_, lo:512], ps.ap[:, lo:512], AF.Exp, [ps.u()], [pT.u(s_)], scale=0.125)
                                if mj is not None:
                                    TTO(pT.ap[:, s_, lo:lo + 128], pT.ap[:, s_, lo:lo + 128], masks.ap[:, mj, :], ALU.mult, [pT.u(s_), masks.u()], [pT.u(s_)])
                                pts.append(s_)
                            if pend is not None:
                                do_pv(pend[0], pend[1] == 0, False)
                            pend = ((slot, bi, lo, pts), it)
                            it += 1
                do_pv(pend[0], pend[1] == 0, True)
                for c in range(2):
                    P.op("dve", "reciprocal", dict(out=fin.ap[:, c, :], in_=pz[c].ap), [pz[c].u()], [fin.u(c)])
                    TTO(fin.ap[:, 2 + c, :], po[c].ap, fin.ap[:, c, :], ALU.mult, [po[c].u(), fin.u(c)], [fin.u(2 + c)])
                STT(fin.ap[:, 4, :], fin.ap[:, 3, :], neglam, fin.ap[:, 2, :], ALU.mult, ALU.add, [fin.u(3), fin.u(2), lw.u()], [fin.u(4)])
                ACT(osq.ap, fin.ap[:, 4, :], AF.Square, [fin.u(4)], [osq.u()])
                pss = s_ps()
                P.mm(pss.ap, [(ones.ap, osq.ap)], reads=[ones.u(), osq.u()], writes=[pss.u()])
                ACT(lnv.ap, pss.ap, AF.Ln, [pss.u(), epsc.u()], [lnv.u()], bias=epsc.ap, scale=1.0 / 128)
                ACT(lnv.ap, lnv.ap, AF.Exp, [lnv.u()], [lnv.u()], scale=-0.5)
                STT(yT.ap[:, hd, q0:q0 + 512], fin.ap[:, 4, :], gsub.ap, lnv.ap, ALU.mult, ALU.mult,
                    [fin.u(4), gsub.u(), lnv.u()], [yT.u((hd, qi))])
        if seg == "B1":
            return yT
        part_B2(yT)

    def part_B2(yT):
        outproj(bwout_d, yT)
        if not SKIP_FFN:
            ffn(w1_post1, w2_post1, G_POST1, TILES_M)
        phase()
        ost = xalloc([128, 2, KT, 512], F32)
        sq = xalloc([128, 16, 512], BF16)
        lnv = xalloc([128, 2, 512], F32)
        epsc = xalloc([128, 1], F32)
        P.op("dve", "memset", dict(ap=epsc.ap, constant=EPS), [], [epsc.u()])
        for ti, (tt, c0, w) in enumerate(TILES_M):
            ss = [(ti * KT + k) % 16 for k in range(KT)]
            for k in range(KT):
                ACT(sq.ap[:, ss[k], :], hT.ap[:, k, c0:c0 + w], AF.Square, [hT.u((k, tt))], [sq.u(ss[k])])
            ps = next_ps()
            mm(ps.u(), ps.ap, [(ones.ap, sq.ap[:, s, :]) for s in ss], [ones.u()] + [sq.u(s) for s in ss])
            l = ti % 2
            ACT(lnv.ap[:, l, :], ps.ap, AF.Ln, [ps.u(), epsc.u()], [lnv.u(l)], bias=epsc.ap, scale=1.0 / D)
            ps2 = next_ps()
            ACT(ps2.ap, lnv.ap[:, l, :], AF.Exp, [lnv.u(l)], [ps2.u()], scale=-0.5)
            for k in range(KT):
                STT(ost.ap[:, l, k, :], hT.ap[:, k, c0:c0 + w], gcol(G_FIN, k), ps2.ap, ALU.mult, ALU.mult,
                    [hT.u((k, tt)), ps2.u(), gains.u()], [ost.u((l, k))])
            P.dma("sp", outT_d[:, :, c0:c0 + w], ost.ap[:, l, :, :], reads=[ost.u((l, k)) for k in range(KT)])
        return None

    res = None
    if doA:
        res = part_A()
        if mode == "A":
            if res is not None and res[0] == "q":
                _, qT, yq = res
                P.dma("sp", q_o, qT.ap, reads=[qT.u((hd, tt)) for hd in range(6) for tt in range(4)])
                P.dma("sp", ym_o, yq.ap, reads=[yq.u((p_, tt)) for p_ in range(2) for tt in range(4)])
            if res is not None and res[0] == "y":
                yT = res[1]
                for k in range(KT):
                    P.op("dve", "tensor_copy", dict(out=hT.ap[:, k, 0:NTOK], in_=yT.ap[:, k, :]),
                         [yT.u((k, tt)) for tt in range(4)] + [hT.u((k, tt)) for tt in range(4)], [hT.u((k, tt)) for tt in range(4)])
            for k in range(KT):
                P.dma("sp", h_o[:, k, :], hT.ap[:, k, 0:NTOK], reads=[hT.u((k, tt)) for tt in range(4)])
    if mode == "B" and seg == "B2":
        load_h()
        for k in range(KT):
            P.dma("sp", uT.ap[:, k, 0:NTOK], y_i[:, k, :], writes=[uT.u((k, tt)) for tt in range(4)])
        part_B2(uT)
    elif mode == "B":
        phase()
        qT = xalloc([128, 6, NTOK], BF16)
        yq = xalloc([128, 2, NTOK], BF16)
        if seg is None:
            load_h()
        P.dma("sp", qT.ap, q_i.get(), writes=[qT.u((hd, tt)) for hd in range(6) for tt in range(4)])
        P.dma("sp", yq.ap, ym_i.get(), writes=[yq.u((p_, tt)) for p_ in range(2) for tt in range(4)])
        r = part_B(qT, yq)
        if seg == "B1":
            for k in range(KT):
                P.dma("sp", y_o[:, k, :], r.ap[:, k, 0:NTOK], reads=[r.u((k, tt)) for tt in range(4)])
        if stop_after == "attn":
            for k in range(KT):
                P.op("dve", "tensor_copy", dict(out=hT.ap[:, k, 0:NTOK], in_=r.ap[:, k, 0:NTOK]),
                     [r.u((k, tt)) for tt in range(4)] + [hT.u((k, tt)) for tt in range(4)], [hT.u((k, tt)) for tt in range(4)])
        if stop_after in ("attn", "mix1"):
            for k in range(KT):
                P.dma("sp", outT_d[:, k, :], hT.ap[:, k, 0:NTOK], reads=[hT.u((k, tt)) for tt in range(4)])

    P.finish()
    with nc.Block() as block:
        P.emit(block)
    es.close()
    return nc, list(used_inputs)


def mt_layout(W):
    K, M = W.shape
    return np.ascontiguousarray(W.reshape(K // 128, 128, M // 128, 128).transpose(2, 1, 0, 3))


def feat_major(a):
    T = a.shape[0]
    return np.ascontiguousarray(a.reshape(T, 8, 128).transpose(2, 1, 0))


def host_prep_A(inp):
    x = np.asarray(inp["x"], np.float32)
    mem = np.asarray(inp["mem"], np.float32)
    pos = np.asarray(inp["positions"], np.int32)
    gl = [inp["ffn_pre_norm"][0], inp["ffn_pre_norm"][1], inp["mix_norm"][0], inp["mix_norm"][1],
          inp["mem_norm"][0], inp["mem_norm"][1], inp["kv_norm"], inp["ffn_post_norm"][0], inp["ffn_post_norm"][1],
          inp["final_norm"]]
    gains = np.concatenate([np.asarray(g, np.float32).reshape(8, 128).T for g in gl], axis=1)
    gains = np.ascontiguousarray(gains)
    convw = np.ascontiguousarray(np.asarray(inp["a_conv_w"][0], np.float32).reshape(3, 6, 128).transpose(2, 1, 0).reshape(128, 18))
    inv_freq = (np.float32(500000.0) ** (-np.arange(0, 16, 2, dtype=np.float32) / np.float32(16))).astype(np.float32)
    invf = np.zeros((128, 1), np.float32)
    rotm = np.zeros((128, 128), np.float32)
    for p in range(128):
        d = p % 64
        if d < 16:
            invf[p, 0] = inv_freq[d % 8]
        if d < 8:
            rotm[p + 8, p] = -1.0
        elif d < 16:
            rotm[p - 8, p] = 1.0
    common = dict(
        gains=gains, convw=convw, invf=invf, rotm=rotm.astype(ml_dtypes.bfloat16),
        w1_pre0=mt_layout(np.asarray(inp["ffn_pre_w1"][0])), w2_pre0=mt_layout(np.asarray(inp["ffn_pre_w2"][0])),
        w1_post0=mt_layout(np.asarray(inp["ffn_post_w1"][0])), w2_post0=mt_layout(np.asarray(inp["ffn_post_w2"][0])),
        w1_pre1=mt_layout(np.asarray(inp["ffn_pre_w1"][1])), w2_pre1=mt_layout(np.asarray(inp["ffn_pre_w2"][1])),
        memw0=mt_layout(np.asarray(inp["mem_w_kv"][0])), memw1=mt_layout(np.asarray(inp["mem_w_kv"][1])),
        a_w_in=mt_layout(np.asarray(inp["a_w_in"][0])), a_w_out=mt_layout(np.asarray(inp["a_w_out"][0])),
        kv_w=mt_layout(np.asarray(inp["kv_w"])), b_w_q=mt_layout(np.asarray(inp["b_w_q"][0])),
    )
    maps = []
    for c in range(8):
        b, j = divmod(c, 4)
        xm = x[b].reshape(64, 128, D)[j::4]
        xl = xm.reshape(NTOK, D)
        halo = np.zeros((16, 2, D), np.float32)
        for lb in range(16):
            g = 4 * lb + j
            if g > 0:
                halo[lb] = x[b, g * 128 - 2:g * 128]
        xa = np.concatenate([xl, halo.reshape(32, D)], axis=0)
        pl = pos[b].reshape(64, 128)[j::4].reshape(1, NTOK)
        m = dict(common)
        m["xT"] = feat_major(xa)
        m["memT"] = feat_major(mem[b])
        m["pos"] = np.ascontiguousarray(np.broadcast_to(pl, (128, NTOK))).astype(np.int32)
        maps.append(m)
    return maps


def host_prep_B(inp, maps):
    lam = np.asarray(inp["b_lambda"][0], np.float32).reshape(1, 256)
    lam128 = np.ascontiguousarray(np.broadcast_to(lam, (128, 256))).astype(np.float32)
    subln = np.ascontiguousarray(np.asarray(inp["b_subln"][0], np.float32).reshape(128, 1))
    common = dict(
        lam=lam128, subln=subln, b_w_out=mt_layout(np.asarray(inp["b_w_out"][0])),
        w1_post1=mt_layout(np.asarray(inp["ffn_post_w1"][1])), w2_post1=mt_layout(np.asarray(inp["ffn_post_w2"][1])),
    )
    tri = (np.arange(128)[:, None] <= np.arange(128)[None, :]).astype(np.float32)
    for c in range(8):
        j = c % 4
        mk = np.zeros((128, 4, 128), np.float32)
        for jp in range(4):
            if jp < j:
                mk[:, jp, :] = 1.0
            elif jp == j:
                mk[:, jp, :] = tri
        maps[c].update(common)
        maps[c]["masks"] = np.ascontiguousarray(mk.reshape(128, 512)).astype(ml_dtypes.bfloat16)
    return maps


_CACHE = {}


def _run(mode, seg, maps):
    key = (mode, seg)
    if key not in _CACHE:
        _CACHE[key] = build(mode, seg=seg)
    nc, used = _CACHE[key]
    in_maps = [{k: m[k] for k in used} for m in maps]
    res = run_bass_kernel_spmd(nc, in_maps, core_ids=list(range(8)))
    return res.results


def kernel(**inputs):
    inp = {k: np.asarray(v) for k, v in inputs.items()}
    maps = host_prep_A(inp)
    maps = host_prep_B(inp, maps)
    r = _run("A", "A1", maps)
    for c in range(8):
        maps[c]["h_in"] = r[c]["h_io"]
    r = _run("A", "A2", maps)
    for c in range(8):
        maps[c]["h_in"] = r[c]["h_io"]
    for b in range(2):
        kg = np.concatenate([r[4 * b + j]["k_loc"] for j in range(4)], axis=0)
        vg = np.concatenate([r[4 * b + j]["v_loc"] for j in range(4)], axis=0)
        for j in range(4):
            maps[4 * b + j]["k_g"] = kg
            maps[4 * b + j]["v_g"] = vg
    r = _run("A", "A3", maps)
    for c in range(8):
        maps[c]["h_in"] = r[c]["h_io"]
        maps[c]["q_in"] = r[c]["q_io"]
        maps[c]["ym_in"] = r[c]["ym_io"]
    r = _run("B", "B1", maps)
    for c in range(8):
        maps[c]["y_in"] = r[c]["y_io"]
    r = _run("B", "B2", maps)
    out = np.zeros((2, 64, 128, D), np.float32)
    for c in range(8):
        b, j = divmod(c, 4)
        o = np.asarray(r[c]["outT"], np.float32).transpose(2, 1, 0).reshape(16, 128, D)
        out[b, j::4] = o
    return out.reshape(2, 8192, D)
```

```python
import math
from contextlib import ExitStack

import numpy as np
import ml_dtypes

import concourse.bass as bass
import concourse.mybir as mybir
from concourse.bass_utils import run_bass_kernel_spmd

F32 = mybir.dt.float32
BF16 = mybir.dt.bfloat16
I32 = mybir.dt.int32
AF = mybir.ActivationFunctionType
ALU = mybir.AluOpType

D = 1024
KT = 8
NTOK = 2048
NHALO = 32
NT = NTOK + NHALO
DFF = 2816
FT = 22
MIX = 768
EPS = 1e-6
LAM_INIT = 0.8 - 0.6 * math.exp(-0.3 * 1)
PI = math.pi
TILES_M = [(0, 0, 512), (1, 512, 512), (2, 1024, 512), (3, 1536, 512)]
TILES_H = [(4, 2048, 32)] + TILES_M
SKIP_FFN = False

G_PRE0, G_PRE1, G_MIX0, G_MIX1, G_MEM0, G_MEM1, G_KV, G_POST0, G_POST1, G_FIN = range(10)


class U:
    __slots__ = ("w", "r")

    def __init__(self):
        self.w = None
        self.r = {}


class Prog:
    NAMES = ["pe", "act", "dve", "pool", "sp"]

    def __init__(self, nc, es, nslots=20):
        self.nc = nc
        self.th = {e: [] for e in self.NAMES}
        self.sem = {e: es.enter_context(nc.semaphore("sem_" + e)) for e in self.NAMES}
        self.cnt = dict.fromkeys(self.NAMES, 0)
        self.seen = {e: {} for e in self.NAMES}
        self.dq = {}
        for q in ("sp", "pool"):
            self.dq[q] = dict(
                sems=[es.enter_context(nc.semaphore(f"dq_{q}{i}")) for i in range(nslots)],
                cnt=[0] * nslots,
                nxt=0,
            )

    def _wait(self, eng, ev):
        key, sem, val = ev
        if self.seen[eng].get(key, 0) >= val:
            return
        self.seen[eng][key] = val
        self.th[eng].append(lambda e, s=sem, v=val: e.wait_ge(s, v))

    def _deps(self, eng, reads, writes):
        evs = []
        for u in reads:
            if u.w is not None:
                evs.append(u.w)
        for u in writes:
            if u.w is not None:
                evs.append(u.w)
            evs.extend(u.r.values())
        for ev in evs:
            if eng == "pe" and ev[0] == "pe":
                continue
            self._wait(eng, ev)

    def _record(self, ev, reads, writes):
        for u in reads:
            old = u.r.get(ev[0])
            if old is None or old[2] < ev[2]:
                u.r[ev[0]] = ev
        for u in writes:
            u.w = ev
            u.r = {}

    def op(self, eng, method, kwargs, reads=(), writes=()):
        self._deps(eng, reads, writes)
        self.cnt[eng] += 1
        sem = self.sem[eng]
        self.th[eng].append(lambda e, m=method, k=kwargs, s=sem: getattr(e, m)(**k).then_inc(s, 1))
        ev = (eng, sem, self.cnt[eng])
        self._record(ev, reads, writes)
        return ev

    def mm(self, out_ap, pairs, reads=(), writes=(), start=True, stop=True):
        self._deps("pe", reads, writes)
        self.cnt["pe"] += 1
        sem = self.sem["pe"]
        n = len(pairs)

        def fn(pe, out_ap=out_ap, pairs=pairs, n=n, start=start, stop=stop, sem=sem):
            ins = None
            for idx, (l, r) in enumerate(pairs):
                ins = pe.matmul(out_ap, l, r, start=(start and idx == 0), stop=(stop and idx == n - 1))
            ins.then_inc(sem, 1)

        self.th["pe"].append(fn)
        ev = ("pe", sem, self.cnt["pe"])
        self._record(ev, reads, writes)
        return ev

    def dma(self, q, out, in_, reads=(), writes=(), **kw):
        d = self.dq[q]
        i = d["nxt"]
        d["nxt"] = (i + 1) % len(d["sems"])
        sem = d["sems"][i]
        key = ("dq", q, i)
        if d["cnt"][i] > 0:
            self._wait(q, (key, sem, d["cnt"][i]))
        self._deps(q, reads, writes)
        d["cnt"][i] += 16
        self.th[q].append(lambda e, o=out, a=in_, s=sem, k=kw: e.dma_start(out=o, in_=a, **k).then_inc(s, 16))
        ev = (key, sem, d["cnt"][i])
        self._record(ev, reads, writes)
        return ev

    def all_events(self):
        evs = [(e, self.sem[e], self.cnt[e]) for e in self.NAMES if self.cnt[e] > 0]
        for q, d in self.dq.items():
            for i, s in enumerate(d["sems"]):
                if d["cnt"][i] > 0:
                    evs.append((("dq", q, i), s, d["cnt"][i]))
        return evs

    def barrier(self, engines=None):
        evs = self.all_events()
        for e in engines or self.NAMES:
            for ev in evs:
                if ev[0] == e:
                    continue
                self._wait(e, ev)

    def wait_event(self, eng, ev):
        self._wait(eng, ev)

    def finish(self):
        for q, d in self.dq.items():
            for i, s in enumerate(d["sems"]):
                if d["cnt"][i] > 0:
                    self._wait(q, (("dq", q, i), s, d["cnt"][i]))
        self.barrier()

    def emit(self, block):
        nc = self.nc

        def run(name):
            def f(eng):
                for t in self.th[name]:
                    t(eng)
            return f

        block.tensor(run("pe"))
        block.scalar(run("act"))
        block.vector(run("dve"))
        block.gpsimd(run("pool"))
        block.sync(run("sp"))


class TT:
    def __init__(self, ap):
        self.ap = ap
        self.units = {}

    def u(self, key=0):
        x = self.units.get(key)
        if x is None:
            x = self.units[key] = U()
        return x


def build(mode, stop_after=None, seg=None):
    nc = bass.Bass("TRN2", target_bir_lowering=False)
    es = ExitStack()
    doA = mode in ("A", "F")
    doB = mode in ("B", "F")

    used_inputs = []

    class LZ:
        def __init__(self, name, shape, dt):
            self.name, self.shape, self.dt, self._ap = name, list(shape), dt, None

        def get(self):
            if self._ap is None:
                self._ap = nc.dram_tensor(self.name, self.shape, self.dt, kind="ExternalInput").ap()
                used_inputs.append(self.name)
            return self._ap

        def __getitem__(self, idx):
            return self.get()[idx]

        def rearrange(self, *a, **k):
            return self.get().rearrange(*a, **k)

    def din(name, shape, dt):
        return LZ(name, shape, dt)

    def dout(name, shape, dt):
        return nc.dram_tensor(name, list(shape), dt, kind="ExternalOutput").ap()

    gains_d = din("gains", [128, 10 * KT], F32)
    xT_d = din("xT", [128, KT, NT], F32)
    memT_d = din("memT", [128, KT, 256], F32)
    pos_d = din("pos", [128, NTOK], I32)
    convw_d = din("convw", [128, 6 * 3], F32)
    invf_d = din("invf", [128, 1], F32)
    rot_d = din("rotm", [128, 128], BF16)
    w1_d = {n: din("w1_" + n, [44, 128, KT, 128], F32) for n in ("pre0", "post0", "pre1")}
    w2_d = {n: din("w2_" + n, [8, 128, FT, 128], F32) for n in ("pre0", "post0", "pre1")}
    memw_d = [din(f"memw{l}", [4, 128, KT, 128], F32) for l in range(2)]
    awin_d = din("a_w_in", [20, 128, KT, 128], F32)
    awout_d = din("a_w_out", [8, 128, KT, 128], F32)
    kvw_d = din("kv_w", [12, 128, KT, 128], F32)
    bwq_d = din("b_w_q", [8, 128, KT, 128], F32)
    mask_d = din("masks", [128, 4 * 128], BF16)
    lam_d = din("lam", [128, 256], F32)
    subln_d = din("subln", [128, 1], F32)
    bwout_d = din("b_w_out", [8, 128, KT, 128], F32)
    w1_post1 = din("w1_post1", [44, 128, KT, 128], F32)
    w2_post1 = din("w2_post1", [8, 128, FT, 128], F32)
    h_i = din("h_in", [128, KT, NTOK], F32)
    y_i = din("y_in", [128, KT, NTOK], BF16)
    q_i = din("q_in", [128, 6, NTOK], BF16)
    ym_i = din("ym_in", [128, 2, NTOK], BF16)
    if mode == "F":
        kloc_d = nc.dram_tensor("k_loc", [128, 6 * NTOK], BF16).ap()
        vloc_d = nc.dram_tensor("v_loc", [128, 6 * NTOK], BF16).ap()
        kg_d = nc.dram_tensor("k_g", [512, 6 * NTOK], BF16).ap()
        vg_d = nc.dram_tensor("v_g", [512, 6 * NTOK], BF16).ap()
    else:
        kg_d = din("k_g", [512, 6 * NTOK], BF16)
        vg_d = din("v_g", [512, 6 * NTOK], BF16)
    if mode == "A":
        h_o = dout("h_io", [128, KT, NTOK], F32)
        if seg in (None, "A2"):
            kloc_d = dout("k_loc", [128, 6 * NTOK], BF16)
            vloc_d = dout("v_loc", [128, 6 * NTOK], BF16)
        if seg in (None, "A3"):
            q_o = dout("q_io", [128, 6, NTOK], BF16)
            ym_o = dout("ym_io", [128, 2, NTOK], BF16)
    if mode == "B":
        if seg == "B1":
            y_o = dout("y_io", [128, KT, NTOK], BF16)
        else:
            outT_d = dout("outT", [128, KT, NTOK], F32)
    if mode == "F":
        outT_d = dout("outT", [128, KT, NTOK], F32)

    def sb(name, shape, dt):
        return es.enter_context(nc.sbuf_tensor(name, list(shape), dt))

    hT = TT(sb("hT", [128, KT, NT], F32)[:])
    uT = TT(sb("uT", [128, KT, NT], BF16)[:])
    gains = TT(sb("gains_sb", [128, 10 * KT], F32)[:])
    ones = TT(sb("ones_sb", [128, 128], BF16)[:])
    NW = 6
    WSLOT = 1408
    wring_t = sb("wring", [128, NW, WSLOT], BF16)
    wring = [TT(wring_t[:, i, :]) for i in range(NW)]
    XW = 92160 // 4
    xreg = sb("xreg", [128, XW], F32)
    psb = [TT(es.enter_context(nc.psum_tensor(f"ps{i}", [128, 512], F32))[:]) for i in range(8)]

    P = Prog(nc, es)
    st = dict(w=0, ps=0, xoff=0)

    def xalloc(shape, dt):
        n = int(np.prod(shape[1:]))
        nbytes = n * (4 if dt in (F32, I32) else 2)
        nbytes = (nbytes + 31) // 32 * 32
        off = st["xoff"]
        assert off + nbytes <= XW * 4, f"xreg overflow {off + nbytes}"
        st["xoff"] = off + nbytes
        ap = xreg[:, off // 4:(off + nbytes) // 4]
        if dt != F32:
            ap = ap.bitcast(dt)
        ap = ap[:, 0:n]
        if len(shape) == 3:
            ap = ap.rearrange("p (a b) -> p a b", b=shape[2])
        elif len(shape) == 4:
            ap = ap.rearrange("p (a b c) -> p a b c", b=shape[2], c=shape[3])
        return TT(ap)

    def phase():
        P.barrier()
        st["xoff"] = 0

    def next_ps(ring=None):
        ring = ring or range(8)
        i = st["ps"]
        st["ps"] = i + 1
        return psb[ring[i % len(ring)]]

    def load_w(dram_mtile, nelem):
        i = st["w"]
        st["w"] = i + 1
        slot = wring[i % NW]
        P.dma("pool", slot.ap[:, 0:nelem], dram_mtile, writes=[slot.u()])
        return slot

    def wflat(d):
        return d.rearrange("p k c -> p (k c)")

    def mm(out_u, out_ap, pairs, reads):
        return P.mm(out_ap, pairs, reads=reads, writes=[out_u])

    def ACT(out, in_, func, reads, writes, **kw):
        P.op("act", "activation", dict(out=out, in_=in_, func=func, **kw), reads, writes)

    def TTO(out, in0, in1, op, reads, writes, eng="dve"):
        P.op(eng, "tensor_tensor", dict(out=out, in0=in0, in1=in1, op=op), reads, writes)

    def STT(out, in0, scalar, in1, op0, op1, reads, writes):
        P.op("dve", "scalar_tensor_tensor", dict(out=out, in0=in0, scalar=scalar, in1=in1, op0=op0, op1=op1), reads, writes)

    def TS(out, in0, s1, s2, op0, op1, reads, writes):
        kw = dict(out=out, in0=in0, scalar1=s1, scalar2=s2, op0=op0)
        if op1 is not None:
            kw["op1"] = op1
        P.op("dve", "tensor_scalar", kw, reads, writes)

    def gcol(g, k):
        return gains.ap[:, g * KT + k:g * KT + k + 1]

    def proj_mm(ps, w, wslot, src, tt, c0, src_k=KT):
        mm(ps.u(), ps.ap[:, 0:w], [(wslot.ap[:, k * 128:(k + 1) * 128], src.ap[:, k, c0:c0 + w]) for k in range(src_k)],
           [src.u((k, tt)) for k in range(src_k)] + [wslot.u()])

    P.dma("sp", gains.ap, gains_d.get(), writes=[gains.u()])
    P.op("dve", "memset", dict(ap=ones.ap, constant=1.0), [], [ones.u()])

    def rmsnorm(src, dst, g, tiles, nk=KT, mean_n=D, gain_col=None, dkey=None):
        sq = xalloc([128, 16, 512], BF16)
        lnv = xalloc([128, 2, 512], F32)
        epsc = xalloc([128, 1], F32)
        P.op("dve", "memset", dict(ap=epsc.ap, constant=EPS), [], [epsc.u()])
        for ti, (tt, c0, w) in enumerate(tiles):
            ss = [(ti * nk + k) % 16 for k in range(nk)]
            for k in range(nk):
                ACT(sq.ap[:, ss[k], 0:w], src.ap[:, k, c0:c0 + w], AF.Square, [src.u((k, tt))], [sq.u(ss[k])])
            ps = next_ps()
            mm(ps.u(), ps.ap[:, 0:w], [(ones.ap, sq.ap[:, s, 0:w]) for s in ss], [ones.u()] + [sq.u(s) for s in ss])
            l = ti % 2
            ACT(lnv.ap[:, l, 0:w], ps.ap[:, 0:w], AF.Ln, [ps.u(), epsc.u()], [lnv.u(l)], bias=epsc.ap, scale=1.0 / mean_n)
            ps2 = next_ps()
            ACT(ps2.ap[:, 0:w], lnv.ap[:, l, 0:w], AF.Exp, [lnv.u(l)], [ps2.u()], scale=-0.5)
            for k in range(nk):
                gc = gcol(g, k) if gain_col is None else gain_col
                STT(dst.ap[:, k, c0:c0 + w], src.ap[:, k, c0:c0 + w], gc, ps2.ap[:, 0:w], ALU.mult, ALU.mult,
                    [src.u((k, tt)), ps2.u(), gains.u()], [dst.u((k, tt))])

    def ffn(w1d, w2d, g, tiles):
        phase()
        rmsnorm(hT, uT, g, tiles)
        hid = xalloc([128, 11, NT], BF16)
        sg = xalloc([128, 3, 512], BF16)
        nsg = 0
        for half in range(2):
            for f in range(11):
                fa = half * 11 + f
                wg = load_w(wflat(w1d[fa]), KT * 128)
                wu = load_w(wflat(w1d[FT + fa]), KT * 128)
                for (tt, c0, w) in tiles:
                    psg = next_ps()
                    psu = next_ps()
                    proj_mm(psg, w, wg, uT, tt, c0)
                    proj_mm(psu, w, wu, uT, tt, c0)
                    s = nsg % 3
                    nsg += 1
                    ACT(sg.ap[:, s, 0:w], psg.ap[:, 0:w], AF.Silu, [psg.u()], [sg.u(s)])
                    TTO(hid.ap[:, f, c0:c0 + w], psu.ap[:, 0:w], sg.ap[:, s, 0:w], ALU.mult, [psu.u(), sg.u(s)], [hid.u((f, tt))])
            for m in range(8):
                w2s = load_w(w2d[m][:, half * 11:(half + 1) * 11, :].rearrange("p f c -> p (f c)"), 11 * 128)
                for (tt, c0, w) in tiles:
                    ps = next_ps()
                    mm(ps.u(), ps.ap[:, 0:w], [(w2s.ap[:, f * 128:(f + 1) * 128], hid.ap[:, f, c0:c0 + w]) for f in range(11)],
                       [hid.u((f, tt)) for f in range(11)] + [w2s.u()])
                    STT(hT.ap[:, m, c0:c0 + w], ps.ap[:, 0:w], 0.5, hT.ap[:, m, c0:c0 + w], ALU.mult, ALU.add,
                        [ps.u(), hT.u((m, tt))], [hT.u((m, tt))])

    def outproj(wd, yT):
        for m in range(8):
            ws = load_w(wflat(wd[m]), KT * 128)
            for (tt, c0, w) in TILES_M:
                ps = next_ps()
                proj_mm(ps, w, ws, yT, tt, c0)
                TTO(hT.ap[:, m, c0:c0 + w], ps.ap[:, 0:w], hT.ap[:, m, c0:c0 + w], ALU.add, [ps.u(), hT.u((m, tt))], [hT.u((m, tt))])

    def mem_kv(l, mkT, mv):
        memT = xalloc([128, KT, 256], F32)
        un = xalloc([128, KT, 256], BF16)
        mw = xalloc([128, 4, KT * 128], BF16)
        P.dma("sp", memT.ap, memT_d.get(), writes=[memT.u((k, 0)) for k in range(KT)])
        for m in range(4):
            P.dma("pool", mw.ap[:, m, :], wflat(memw_d[l][m]), writes=[mw.u(m)])
        rmsnorm(memT, un, G_MEM0 + l, [(0, 0, 256)])
        for pair in range(2):
            ps = next_ps()
            mm(ps.u(), ps.ap[:, 0:256], [(mw.ap[:, pair, k * 128:(k + 1) * 128], un.ap[:, k, :]) for k in range(KT)],
               [un.u((k, 0)) for k in range(KT)] + [mw.u(pair)])
            ACT(mkT.ap[:, pair, :], ps.ap[:, 0:256], AF.Copy, [ps.u()], [mkT.u(pair)])
        mwv = mw.ap.rearrange("p m (k c) -> p m k c", c=128)
        for mt in range(2):
            ps = next_ps()
            mm(ps.u(), ps.ap[:, 0:256].rearrange("p (a b) -> p a b", b=128),
               [(un.ap[:, k, mt * 128:(mt + 1) * 128], mwv[:, 2:4, k, :]) for k in range(KT)],
               [un.u((k, 0)) for k in range(KT)] + [mw.u(2), mw.u(3)])
            ACT(mv.ap[:, mt, :], ps.ap[:, 0:256], AF.Copy, [ps.u()], [mv.u(mt)])

    def mem_attn(yap, yunit, mkT, mv):
        pT = xalloc([128, 4, 512], BF16)
        rc = xalloc([128, 2, 512], F32)
        qp = xalloc([128, 2, 2, 512], BF16)
        P.op("dve", "memset", dict(ap=qp.ap.rearrange("p a b c -> p (a b c)"), constant=0.0), [], [qp.u(0), qp.u(1)])
        npt = 0
        nrc = 0
        for (tt, c0, w) in TILES_M:
            for pair in range(2):
                r = nrc % 2
                nrc += 1
                for hh in range(2):
                    p0 = hh * 64
                    P.op("dve", "tensor_copy", dict(out=qp.ap[p0:p0 + 64, r, hh, 0:w], in_=yap[p0:p0 + 64, pair, c0:c0 + w]),
                         [yunit(pair, tt)], [qp.u(r)])
                psos, psss = [], []
                for hh in range(2):
                    pts = []
                    for mt in range(2):
                        ps = next_ps()
                        mm(ps.u(), ps.ap[:, 0:w], [(mkT.ap[:, pair, mt * 128:(mt + 1) * 128], qp.ap[:, r, hh, 0:w])],
                           [mkT.u(pair), qp.u(r)])
                        s = npt % 4
                        npt += 1
                        ACT(pT.ap[:, s, 0:w], ps.ap[:, 0:w], AF.Exp, [ps.u()], [pT.u(s)], scale=0.125)
                        pts.append(s)
                    pso = next_ps()
                    pss = next_ps()
                    mm(pso.u(), pso.ap[:, 0:w], [(mv.ap[:, mt, pair * 128:(pair + 1) * 128], pT.ap[:, pts[mt], 0:w]) for mt in range(2)],
                       [mv.u(0), mv.u(1), pT.u(pts[0]), pT.u(pts[1])])
                    mm(pss.u(), pss.ap[:, 0:w], [(ones.ap, pT.ap[:, pts[mt], 0:w]) for mt in range(2)],
                       [ones.u(), pT.u(pts[0]), pT.u(pts[1])])
                    psos.append(pso)
                    psss.append(pss)
                for hh in range(2):
                    p0 = hh * 64
                    P.op("dve", "reciprocal", dict(out=rc.ap[p0:p0 + 64, r, 0:w], in_=psss[hh].ap[p0:p0 + 64, 0:w]),
                         [psss[hh].u()], [rc.u((r, hh))])
                    TTO(yap[p0:p0 + 64, pair, c0:c0 + w], psos[hh].ap[p0:p0 + 64, 0:w], rc.ap[p0:p0 + 64, r, 0:w], ALU.mult,
                        [psos[hh].u(), rc.u((r, hh)), qp.u(r)], [yunit(pair, tt)])

    def rope_tables():
        Ct = xalloc([128, NTOK], F32)
        St = xalloc([128, NTOK], F32)
        invf = xalloc([128, 1], F32)
        mark = st["xoff"]
        posi = xalloc([128, NTOK], I32)
        ang = xalloc([128, NTOK], F32)
        kf = xalloc([128, NTOK], F32)
        P.dma("sp", posi.ap, pos_d.get(), writes=[posi.u()])
        P.dma("sp", invf.ap, invf_d.get(), writes=[invf.u()])
        P.op("dve", "tensor_copy", dict(out=ang.ap, in_=posi.ap), [posi.u()], [ang.u()])
        TS(ang.ap, ang.ap, invf.ap, None, ALU.mult, None, [ang.u(), invf.u()], [ang.u()])
        TS(posi.ap, ang.ap, 1.0 / (2 * PI), None, ALU.mult, None, [ang.u()], [posi.u()])
        P.op("dve", "tensor_copy", dict(out=kf.ap, in_=posi.ap), [posi.u()], [kf.u()])
        C1 = 6.28125
        C2 = 2 * PI - C1
        for cc in (C1, C2):
            STT(ang.ap, kf.ap, -cc, ang.ap, ALU.mult, ALU.add, [kf.u(), ang.u()], [ang.u()])
        for dst, shift in ((St, 0.0), (Ct, PI / 2)):
            TS(kf.ap, ang.ap, shift, PI, ALU.add, ALU.is_gt, [ang.u()], [kf.u()])
            STT(dst.ap, kf.ap, -2 * PI, ang.ap, ALU.mult, ALU.add, [kf.u(), ang.u()], [dst.u()])
            TS(dst.ap, dst.ap, shift, -PI, ALU.add, ALU.max, [dst.u()], [dst.u()])
            TS(dst.ap, dst.ap, PI, None, ALU.min, None, [dst.u()], [dst.u()])
            ACT(dst.ap, dst.ap, AF.Sin, [dst.u()], [dst.u()])
        P.barrier()
        st["xoff"] = mark
        return Ct, St

    def rope_evac(ps, Ct, St, rotm, ksb, t12, dst_ap, c0, w, dst_units, n):
        s = n % 2
        P.op("dve", "tensor_copy", dict(out=ksb.ap[:, s, 0:w], in_=ps.ap[:, 0:w]), [ps.u()], [ksb.u(s)])
        ps2 = next_ps()
        mm(ps2.u(), ps2.ap[:, 0:w], [(rotm.ap, ksb.ap[:, s, 0:w])], [rotm.u(), ksb.u(s)])
        TTO(t12.ap[:, 2 * s, 0:w], ps.ap[:, 0:w], Ct.ap[:, c0:c0 + w], ALU.mult, [ps.u(), Ct.u()], [t12.u(2 * s)])
        TTO(t12.ap[:, 2 * s + 1, 0:w], ps2.ap[:, 0:w], St.ap[:, c0:c0 + w], ALU.mult, [ps2.u(), St.u()], [t12.u(2 * s + 1)])
        TTO(dst_ap, t12.ap[:, 2 * s, 0:w], t12.ap[:, 2 * s + 1, 0:w], ALU.add, [t12.u(2 * s), t12.u(2 * s + 1)], dst_units)

    def load_h():
        for k in range(KT):
            P.dma("sp", hT.ap[:, k, 0:NTOK], h_i[:, k, :], writes=[hT.u((k, tt)) for tt in range(4)])

    def part_A():
        if seg in (None, "A1"):
            r = part_A1()
            if seg == "A1" or stop_after is not None:
                return r
        if seg == "A2":
            load_h()
        if seg in (None, "A2"):
            r = part_A2()
            if seg == "A2" or stop_after is not None:
                return r
        if seg == "A3":
            load_h()
        return part_A3()

    def part_A1():
        for k in range(KT):
            P.dma("sp", hT.ap[:, k, :], xT_d[:, k, :], writes=[hT.u((k, tt)) for tt in range(5)])
        if not SKIP_FFN:
            ffn(w1_d["pre0"], w2_d["pre0"], G_PRE0, TILES_H)
        if stop_after == "ffn0":
            return None
        phase()
        mkT = xalloc([128, 2, 256], BF16)
        mv = xalloc([128, 2, 256], BF16)
        convw = xalloc([128, 18], F32)
        P.dma("sp", convw.ap, convw_d.get(), writes=[convw.u()])
        keep = st["xoff"]
        mem_kv(0, mkT, mv)
        P.barrier()
        if stop_after == "memkv":
            return None
        st["xoff"] = keep
        rmsnorm(hT, uT, G_MIX0, TILES_H)
        P.barrier()
        st["xoff"] = keep
        yT = xalloc([128, KT, NTOK], BF16)
        zb = xalloc([128, 2, 16, 130], F32)
        csb = xalloc([128, 2, 512], F32)
        acc = xalloc([128, 2, 512], F32)
        nz = 0
        for i in range(6):
            wb = load_w(wflat(awin_d[i]), KT * 128)
            wc = load_w(wflat(awin_d[6 + i]), KT * 128)
            wx = load_w(wflat(awin_d[12 + i]), KT * 128)
            zi = i % 2
            for (tt, c0, w) in TILES_H:
                psc = next_ps()
                psx = next_ps()
                proj_mm(psc, w, wc, uT, tt, c0)
                proj_mm(psx, w, wx, uT, tt, c0)
                s = nz % 2
                nz += 1
                ACT(csb.ap[:, s, 0:w], psc.ap[:, 0:w], AF.Copy, [psc.u()], [csb.u(s)])
                if tt == 4:
                    TTO(zb.ap[:, zi, :, 0:2], psx.ap[:, 0:32].rearrange("p (a b) -> p a b", b=2),
                        csb.ap[:, s, 0:32].rearrange("p (a b) -> p a b", b=2), ALU.mult,
                        [psx.u(), csb.u(s)], [zb.u((zi, t, "h")) for t in range(4)])
                    continue
                psbg = next_ps()
                proj_mm(psbg, w, wb, uT, tt, c0)
                zu = zb.u((zi, tt))
                zh = zb.u((zi, tt, "h"))
                zv = zb.ap[:, zi, 4 * tt:4 * tt + 4, :]
                TTO(zv[:, :, 2:130], psx.ap[:, 0:512].rearrange("p (a b) -> p a b", b=128),
                    csb.ap[:, s, 0:512].rearrange("p (a b) -> p a b", b=128), ALU.mult, [psx.u(), csb.u(s)], [zu])
                a3 = acc.ap[:, s, :].rearrange("p (a b) -> p a b", b=128)
                TS(a3, zv[:, :, 0:128], convw.ap[:, i * 3:i * 3 + 1], None, ALU.mult, None, [zu, zh, convw.u()], [acc.u(s)])
                for j in (1, 2):
                    STT(a3, zv[:, :, j:j + 128], convw.ap[:, i * 3 + j:i * 3 + j + 1], a3, ALU.mult, ALU.add,
                        [zu, zh, convw.u(), acc.u(s)], [acc.u(s)])
                TTO(yT.ap[:, i, c0:c0 + 512], psbg.ap[:, 0:512], acc.ap[:, s, :], ALU.mult, [psbg.u(), acc.u(s)], [yT.u((i, tt))])
        if stop_after == "conv":
            return None
        for pair in range(2):
            wq = load_w(wflat(awin_d[18 + pair]), KT * 128)
            for (tt, c0, w) in TILES_M:
                ps = next_ps()
                proj_mm(ps, w, wq, uT, tt, c0)
                ACT(yT.ap[:, 6 + pair, c0:c0 + w], ps.ap[:, 0:w], AF.Copy, [ps.u()], [yT.u((6 + pair, tt))])
        if stop_after == "qmem":
            return None
        mem_attn(yT.ap[:, 6:8, :], lambda pair, tt: yT.u((6 + pair, tt)), mkT, mv)
        if stop_after == "ymix0":
            return ("y", yT)
        outproj(awout_d, yT)
        return None

    def part_A2():
        if not SKIP_FFN:
            ffn(w1_d["post0"], w2_d["post0"], G_POST0, TILES_M)
        if stop_after == "post0":
            return None
        phase()
        rmsnorm(hT, uT, G_KV, TILES_M)
        P.barrier()
        st["xoff"] = 0
        if stop_after == "kvnorm":
            return None
        Ct, St = rope_tables()
        if stop_after == "tables":
            return None
        rotm = xalloc([128, 128], BF16)
        P.dma("sp", rotm.ap, rot_d.get(), writes=[rotm.u()])
        ksb = xalloc([128, 2, 512], BF16)
        t12 = xalloc([128, 4, 512], F32)
        kst = xalloc([128, 2, NTOK], BF16)
        vw = xalloc([128, 6, KT * 128], BF16)
        vst = xalloc([128, 3, 768], BF16)
        nr = 0
        for hd in range(6):
            wk = load_w(wflat(kvw_d[hd]), KT * 128)
            ks = hd % 2
            for (tt, c0, w) in TILES_M:
                ps = next_ps()
                proj_mm(ps, w, wk, uT, tt, c0)
                rope_evac(ps, Ct, St, rotm, ksb, t12, kst.ap[:, ks, c0:c0 + w], c0, w, [kst.u((ks, tt))], nr)
                nr += 1
            P.dma("sp", kloc_d[:, hd * NTOK:(hd + 1) * NTOK], kst.ap[:, ks, :], reads=[kst.u((ks, tt)) for tt in range(4)])
        if stop_after == "kproj":
            return None
        for m in range(6):
            P.dma("pool", vw.ap[:, m, :], wflat(kvw_d[6 + m]), writes=[vw.u(m)])
        vwv = vw.ap.rearrange("p m (k c) -> p m k c", c=128)
        vloc_v = vloc_d.rearrange("p (h l e) -> p h l e", h=6, l=16)
        for lb in range(16):
            tt = lb // 4
            vs = lb % 3
            for half in range(2):
                ps = next_ps()
                mm(ps.u(), ps.ap[:, 0:384].rearrange("p (a b) -> p a b", b=128),
                   [(uT.ap[:, k, lb * 128:(lb + 1) * 128], vwv[:, 3 * half:3 * half + 3, k, :]) for k in range(KT)],
                   [uT.u((k, tt)) for k in range(KT)] + [vw.u(3 * half + x) for x in range(3)])
                ACT(vst.ap[:, vs, half * 384:(half + 1) * 384], ps.ap[:, 0:384], AF.Copy, [ps.u()], [vst.u((vs, half))])
            P.dma("sp", vloc_v[:, :, lb, :], vst.ap[:, vs, :].rearrange("p (h e) -> p h e", e=128),
                  reads=[vst.u((vs, 0)), vst.u((vs, 1))])
        return None

    def part_A3():
        if not SKIP_FFN:
            ffn(w1_d["pre1"], w2_d["pre1"], G_PRE1, TILES_M)
        phase()
        qT = xalloc([128, 6, NTOK], BF16)
        yq = xalloc([128, 2, NTOK], BF16)
        mkT = xalloc([128, 2, 256], BF16)
        mv = xalloc([128, 2, 256], BF16)
        keep = st["xoff"]
        mem_kv(1, mkT, mv)
        P.barrier()
        st["xoff"] = keep
        rmsnorm(hT, uT, G_MIX1, TILES_M)
        P.barrier()
        st["xoff"] = keep
        Ct, St = rope_tables()
        rotm = xalloc([128, 128], BF16)
        P.dma("sp", rotm.ap, rot_d.get(), writes=[rotm.u()])
        ksb = xalloc([128, 2, 512], BF16)
        t12 = xalloc([128, 4, 512], F32)
        nr = 0
        for hd in range(6):
            wq = load_w(wflat(bwq_d[hd]), KT * 128)
            for (tt, c0, w) in TILES_M:
                ps = next_ps()
                proj_mm(ps, w, wq, uT, tt, c0)
                rope_evac(ps, Ct, St, rotm, ksb, t12, qT.ap[:, hd, c0:c0 + w], c0, w, [qT.u((hd, tt))], nr)
                nr += 1
        for pair in range(2):
            wq = load_w(wflat(bwq_d[6 + pair]), KT * 128)
            for (tt, c0, w) in TILES_M:
                ps = next_ps()
                proj_mm(ps, w, wq, uT, tt, c0)
                ACT(yq.ap[:, pair, c0:c0 + w], ps.ap[:, 0:w], AF.Copy, [ps.u()], [yq.u((pair, tt))])
        mem_attn(yq.ap, lambda pair, tt: yq.u((pair, tt)), mkT, mv)
        return ("q", qT, yq)

    def part_B(qT, yq):
        yT = uT
        masks = xalloc([128, 4, 128], BF16)
        lam = xalloc([128, 256], F32)
        lw = xalloc([128, 8], F32)
        gsub = xalloc([128, 1], F32)
        epsc = xalloc([128, 1], F32)
        P.dma("sp", masks.ap.rearrange("p a b -> p (a b)"), mask_d.get(), writes=[masks.u()])
        P.dma("sp", lam.ap, lam_d.get(), writes=[lam.u()])
        P.dma("sp", gsub.ap, subln_d.get(), writes=[gsub.u()])
        P.op("dve", "memset", dict(ap=epsc.ap, constant=EPS), [], [epsc.u()])
        for pair in range(2):
            for (tt, c0, w) in TILES_M:
                P.op("dve", "tensor_copy", dict(out=yT.ap[:, 6 + pair, c0:c0 + w], in_=yq.ap[:, pair, c0:c0 + w]),
                     [yq.u((pair, tt))], [yT.u((6 + pair, tt))])
        for i in range(2):
            P.op("dve", "tensor_tensor", dict(out=lam.ap[:, 128 * i:128 * i + 64], in0=lam.ap[:, 128 * i:128 * i + 64],
                                              in1=lam.ap[:, 128 * i + 64:128 * i + 128], op=ALU.mult), [lam.u()], [lam.u()])
            P.op("dve", "reduce_sum", dict(out=lw.ap[:, i:i + 1], in_=lam.ap[:, 128 * i:128 * i + 64], axis=mybir.AxisListType.X),
                 [lam.u()], [lw.u()])
        ACT(lw.ap[:, 2:4], lw.ap[:, 0:2], AF.Exp, [lw.u()], [lw.u()])
        TTO(lw.ap[:, 4:5], lw.ap[:, 3:4], lw.ap[:, 2:3], ALU.subtract, [lw.u()], [lw.u()])
        TS(lw.ap[:, 5:6], lw.ap[:, 4:5], -LAM_INIT, None, ALU.add, None, [lw.u()], [lw.u()])
        neglam = lw.ap[:, 5:6]
        TS(gsub.ap, gsub.ap, 1.0 - LAM_INIT, None, ALU.mult, None, [gsub.u()], [gsub.u()])

        NCH = 4
        kch = xalloc([128, NCH, 2048], BF16)
        vch = xalloc([128, NCH, 16, 128], BF16)
        pT = xalloc([128, 6, 512], BF16)
        fin = xalloc([128, 5, 512], F32)
        osq = xalloc([128, 512], BF16)
        lnv = xalloc([128, 512], F32)
        kg_v = kg_d.rearrange("(r p) (h t) -> p r h t", p=128, h=6)
        vg_v = vg_d.rearrange("(r p) (h t) -> p r h t", p=128, h=6)
        S_RING = [0, 1, 2, 3]
        npt = 0
        nps = [0]

        def s_ps():
            i = nps[0]
            nps[0] += 1
            return psb[S_RING[i % 4]]

        jobs = [(hd, qi, ci) for hd in range(6) for qi in range(4) for ci in range(qi + 1)]
        qpd = xalloc([128, 2, 2, 512], BF16)
        P.op("dve", "memset", dict(ap=qpd.ap.rearrange("p a b c -> p (a b c)"), constant=0.0), [], [qpd.u(0), qpd.u(1)])
        nqp = 0

        def load_chunk(n):
            hd, qi, ci = jobs[n]
            slot = n % NCH
            P.dma("sp", kch.ap[:, slot, :].rearrange("p (r t) -> p r t", r=4), kg_v[:, :, hd, ci * 512:(ci + 1) * 512],
                  writes=[kch.u(slot)])
            P.dma("sp", vch.ap[:, slot, :, :].rearrange("p (r l) e -> p r (l e)", r=4), vg_v[:, :, hd, ci * 512:(ci + 1) * 512],
                  writes=[vch.u(slot)])

        load_chunk(0)
        load_chunk(1)
        jn = 0
        for hd in range(6):
            for qi in range(4):
                q0 = qi * 512
                po = [psb[4], psb[5]]
                pz = [psb[6], psb[7]]
                pend = None
                qb = nqp % 2
                nqp += 1
                for c in range(2):
                    P.op("dve", "tensor_copy", dict(out=qpd.ap[64 * c:64 * c + 64, qb, c, :], in_=qT.ap[64 * c:64 * c + 64, hd, q0:q0 + 512]),
                         [qT.u((hd, qi))], [qpd.u(qb)])

                def do_pv(pend, first, last):
                    slot, bi, lo, pts = pend
                    for c in range(2):
                        P.mm(po[c].ap[:, lo:512], [(vch.ap[:, slot, bi, :], pT.ap[:, pts[c], lo:512])],
                             reads=[vch.u(slot), pT.u(pts[c])], writes=[po[c].u()], start=first, stop=last)
                        P.mm(pz[c].ap[:, lo:512], [(ones.ap, pT.ap[:, pts[c], lo:512])],
                             reads=[ones.u(), pT.u(pts[c])], writes=[pz[c].u()], start=first, stop=last)

                it = 0
                for ci in range(qi + 1):
                    assert jobs[jn] == (hd, qi, ci)
                    slot = jn % NCH
                    if jn + 2 < len(jobs):
                        load_chunk(jn + 2)
                    jn += 1
                    for jp in range(4):
                        for lbl in range(4):
                            bi = jp * 4 + lbl
                            lo, mj = (0, None) if ci < qi else (128 * lbl, jp)
                            pts = []
                            for c in range(2):
                                ps = s_ps()
                                p0 = 64 * c
                                P.mm(ps.ap[:, lo:512], [(kch.ap[:, slot, bi * 128:(bi + 1) * 128], qpd.ap[:, qb, c, lo:512])],
                                     reads=[kch.u(slot), qpd.u(qb)], writes=[ps.u()])
                                s_ = npt % 6
                                npt += 1
                                ACT(pT.ap[:, s_, lo:512], ps.ap[:, lo:512], AF.Exp, [ps.u()], [pT.u(s_)], scale=0.125)
                                if mj is not None:
                                    TTO(pT.ap[:, s_, lo:lo + 128], pT.ap[:, s_, lo:lo + 128], masks.ap[:, mj, :], ALU.mult, [pT.u(s_), masks.u()], [pT.u(s_)])
                                pts.append(s_)
                            if pend is not None:
                                do_pv(pend[0], pend[1] == 0, False)
                            pend = ((slot, bi, lo, pts), it)
                            it += 1
                do_pv(pend[0], pend[1] == 0, True)
                for c in range(2):
                    P.op("dve", "reciprocal", dict(out=fin.ap[:, c, :], in_=pz[c].ap), [pz[c].u()], [fin.u(c)])
                    TTO(fin.ap[:, 2 + c, :], po[c].ap, fin.ap[:, c, :], ALU.mult, [po[c].u(), fin.u(c)], [fin.u(2 + c)])
                STT(fin.ap[:, 4, :], fin.ap[:, 3, :], neglam, fin.ap[:, 2, :], ALU.mult, ALU.add, [fin.u(3), fin.u(2), lw.u()], [fin.u(4)])
                ACT(osq.ap, fin.ap[:, 4, :], AF.Square, [fin.u(4)], [osq.u()])
                pss = s_ps()
                P.mm(pss.ap, [(ones.ap, osq.ap)], reads=[ones.u(), osq.u()], writes=[pss.u()])
                ACT(lnv.ap, pss.ap, AF.Ln, [pss.u(), epsc.u()], [lnv.u()], bias=epsc.ap, scale=1.0 / 128)
                ACT(lnv.ap, lnv.ap, AF.Exp, [lnv.u()], [lnv.u()], scale=-0.5)
                STT(yT.ap[:, hd, q0:q0 + 512], fin.ap[:, 4, :], gsub.ap, lnv.ap, ALU.mult, ALU.mult,
                    [fin.u(4), gsub.u(), lnv.u()], [yT.u((hd, qi))])
        if seg == "B1":
            return yT
        part_B2(yT)

    def part_B2(yT):
        outproj(bwout_d, yT)
        if not SKIP_FFN:
            ffn(w1_post1, w2_post1, G_POST1, TILES_M)
        phase()
        ost = xalloc([128, 2, KT, 512], F32)
        sq = xalloc([128, 16, 512], BF16)
        lnv = xalloc([128, 2, 512], F32)
        epsc = xalloc([128, 1], F32)
        P.op("dve", "memset", dict(ap=epsc.ap, constant=EPS), [], [epsc.u()])
        for ti, (tt, c0, w) in enumerate(TILES_M):
            ss = [(ti * KT + k) % 16 for k in range(KT)]
            for k in range(KT):
                ACT(sq.ap[:, ss[k], :], hT.ap[:, k, c0:c0 + w], AF.Square, [hT.u((k, tt))], [sq.u(ss[k])])
            ps = next_ps()
            mm(ps.u(), ps.ap, [(ones.ap, sq.ap[:, s, :]) for s in ss], [ones.u()] + [sq.u(s) for s in ss])
            l = ti % 2
            ACT(lnv.ap[:, l, :], ps.ap, AF.Ln, [ps.u(), epsc.u()], [lnv.u(l)], bias=epsc.ap, scale=1.0 / D)
            ps2 = next_ps()
            ACT(ps2.ap, lnv.ap[:, l, :], AF.Exp, [lnv.u(l)], [ps2.u()], scale=-0.5)
            for k in range(KT):
                STT(ost.ap[:, l, k, :], hT.ap[:, k, c0:c0 + w], gcol(G_FIN, k), ps2.ap, ALU.mult, ALU.mult,
                    [hT.u((k, tt)), ps2.u(), gains.u()], [ost.u((l, k))])
            P.dma("sp", outT_d[:, :, c0:c0 + w], ost.ap[:, l, :, :], reads=[ost.u((l, k)) for k in range(KT)])
        return None

    res = None
    if doA:
        res = part_A()
        if mode == "A":
            if res is not None and res[0] == "q":
                _, qT, yq = res
                P.dma("sp", q_o, qT.ap, reads=[qT.u((hd, tt)) for hd in range(6) for tt in range(4)])
                P.dma("sp", ym_o, yq.ap, reads=[yq.u((p_, tt)) for p_ in range(2) for tt in range(4)])
            if res is not None and res[0] == "y":
                yT = res[1]
                for k in range(KT):
                    P.op("dve", "tensor_copy", dict(out=hT.ap[:, k, 0:NTOK], in_=yT.ap[:, k, :]),
                         [yT.u((k, tt)) for tt in range(4)] + [hT.u((k, tt)) for tt in range(4)], [hT.u((k, tt)) for tt in range(4)])
            for k in range(KT):
                P.dma("sp", h_o[:, k, :], hT.ap[:, k, 0:NTOK], reads=[hT.u((k, tt)) for tt in range(4)])
    if mode == "B" and seg == "B2":
        load_h()
        for k in range(KT):
            P.dma("sp", uT.ap[:, k, 0:NTOK], y_i[:, k, :], writes=[uT.u((k, tt)) for tt in range(4)])
        part_B2(uT)
    elif mode == "B":
        phase()
        qT = xalloc([128, 6, NTOK], BF16)
        yq = xalloc([128, 2, NTOK], BF16)
        if seg is None:
            load_h()
        P.dma("sp", qT.ap, q_i.get(), writes=[qT.u((hd, tt)) for hd in range(6) for tt in range(4)])
        P.dma("sp", yq.ap, ym_i.get(), writes=[yq.u((p_, tt)) for p_ in range(2) for tt in range(4)])
        r = part_B(qT, yq)
        if seg == "B1":
            for k in range(KT):
                P.dma("sp", y_o[:, k, :], r.ap[:, k, 0:NTOK], reads=[r.u((k, tt)) for tt in range(4)])
        if stop_after == "attn":
            for k in range(KT):
                P.op("dve", "tensor_copy", dict(out=hT.ap[:, k, 0:NTOK], in_=r.ap[:, k, 0:NTOK]),
                     [r.u((k, tt)) for tt in range(4)] + [hT.u((k, tt)) for tt in range(4)], [hT.u((k, tt)) for tt in range(4)])
        if stop_after in ("attn", "mix1"):
            for k in range(KT):
                P.dma("sp", outT_d[:, k, :], hT.ap[:, k, 0:NTOK], reads=[hT.u((k, tt)) for tt in range(4)])

    P.finish()
    with nc.Block() as block:
        P.emit(block)
    es.close()
    return nc, list(used_inputs)


def mt_layout(W):
    K, M = W.shape
    return np.ascontiguousarray(W.reshape(K // 128, 128, M // 128, 128).transpose(2, 1, 0, 3))


def feat_major(a):
    T = a.shape[0]
    return np.ascontiguousarray(a.reshape(T, 8, 128).transpose(2, 1, 0))


def host_prep_A(inp):
    x = np.asarray(inp["x"], np.float32)
    mem = np.asarray(inp["mem"], np.float32)
    pos = np.asarray(inp["positions"], np.int32)
    gl = [inp["ffn_pre_norm"][0], inp["ffn_pre_norm"][1], inp["mix_norm"][0], inp["mix_norm"][1],
          inp["mem_norm"][0], inp["mem_norm"][1], inp["kv_norm"], inp["ffn_post_norm"][0], inp["ffn_post_norm"][1],
          inp["final_norm"]]
    gains = np.concatenate([np.asarray(g, np.float32).reshape(8, 128).T for g in gl], axis=1)
    gains = np.ascontiguousarray(gains)
    convw = np.ascontiguousarray(np.asarray(inp["a_conv_w"][0], np.float32).reshape(3, 6, 128).transpose(2, 1, 0).reshape(128, 18))
    inv_freq = (np.float32(500000.0) ** (-np.arange(0, 16, 2, dtype=np.float32) / np.float32(16))).astype(np.float32)
    invf = np.zeros((128, 1), np.float32)
    rotm = np.zeros((128, 128), np.float32)
    for p in range(128):
        d = p % 64
        if d < 16:
            invf[p, 0] = inv_freq[d % 8]
        if d < 8:
            rotm[p + 8, p] = -1.0
        elif d < 16:
            rotm[p - 8, p] = 1.0
    common = dict(
        gains=gains, convw=convw, invf=invf, rotm=rotm.astype(ml_dtypes.bfloat16),
        w1_pre0=mt_layout(np.asarray(inp["ffn_pre_w1"][0])), w2_pre0=mt_layout(np.asarray(inp["ffn_pre_w2"][0])),
        w1_post0=mt_layout(np.asarray(inp["ffn_post_w1"][0])), w2_post0=mt_layout(np.asarray(inp["ffn_post_w2"][0])),
        w1_pre1=mt_layout(np.asarray(inp["ffn_pre_w1"][1])), w2_pre1=mt_layout(np.asarray(inp["ffn_pre_w2"][1])),
        memw0=mt_layout(np.asarray(inp["mem_w_kv"][0])), memw1=mt_layout(np.asarray(inp["mem_w_kv"][1])),
        a_w_in=mt_layout(np.asarray(inp["a_w_in"][0])), a_w_out=mt_layout(np.asarray(inp["a_w_out"][0])),
        kv_w=mt_layout(np.asarray(inp["kv_w"])), b_w_q=mt_layout(np.asarray(inp["b_w_q"][0])),
    )
    maps = []
    for c in range(8):
        b, j = divmod(c, 4)
        xm = x[b].reshape(64, 128, D)[j::4]
        xl = xm.reshape(NTOK, D)
        halo = np.zeros((16, 2, D), np.float32)
        for lb in range(16):
            g = 4 * lb + j
            if g > 0:
                halo[lb] = x[b, g * 128 - 2:g * 128]
        xa = np.concatenate([xl, halo.reshape(32, D)], axis=0)
        pl = pos[b].reshape(64, 128)[j::4].reshape(1, NTOK)
        m = dict(common)
        m["xT"] = feat_major(xa)
        m["memT"] = feat_major(mem[b])
        m["pos"] = np.ascontiguousarray(np.broadcast_to(pl, (128, NTOK))).astype(np.int32)
        maps.append(m)
    return maps


def host_prep_B(inp, maps):
    lam = np.asarray(inp["b_lambda"][0], np.float32).reshape(1, 256)
    lam128 = np.ascontiguousarray(np.broadcast_to(lam, (128, 256))).astype(np.float32)
    subln = np.ascontiguousarray(np.asarray(inp["b_subln"][0], np.float32).reshape(128, 1))
    common = dict(
        lam=lam128, subln=subln, b_w_out=mt_layout(np.asarray(inp["b_w_out"][0])),
        w1_post1=mt_layout(np.asarray(inp["ffn_post_w1"][1])), w2_post1=mt_layout(np.asarray(inp["ffn_post_w2"][1])),
    )
    tri = (np.arange(128)[:, None] <= np.arange(128)[None, :]).astype(np.float32)
    for c in range(8):
        j = c % 4
        mk = np.zeros((128, 4, 128), np.float32)
        for jp in range(4):
            if jp < j:
                mk[:, jp, :] = 1.0
            elif jp == j:
                mk[:, jp, :] = tri
        maps[c].update(common)
        maps[c]["masks"] = np.ascontiguousarray(mk.reshape(128, 512)).astype(ml_dtypes.bfloat16)
    return maps


_CACHE = {}


def _run(mode, seg, maps):
    key = (mode, seg)
    if key not in _CACHE:
        _CACHE[key] = build(mode, seg=seg)
    nc, used = _CACHE[key]
    in_maps = [{k: m[k] for k in used} for m in maps]
    res = run_bass_kernel_spmd(nc, in_maps, core_ids=list(range(8)))
    return res.results


def kernel(**inputs):
    inp = {k: np.asarray(v) for k, v in inputs.items()}
    maps = host_prep_A(inp)
    maps = host_prep_B(inp, maps)
    r = _run("A", None, maps)
    for c in range(8):
        maps[c]["h_in"] = r[c]["h_io"]
        maps[c]["q_in"] = r[c]["q_io"]
        maps[c]["ym_in"] = r[c]["ym_io"]
    for b in range(2):
        kg = np.concatenate([r[4 * b + j]["k_loc"] for j in range(4)], axis=0)
        vg = np.concatenate([r[4 * b + j]["v_loc"] for j in range(4)], axis=0)
        for j in range(4):
            maps[4 * b + j]["k_g"] = kg
            maps[4 * b + j]["v_g"] = vg
    r = _run("B", None, maps)
    out = np.zeros((2, 64, 128, D), np.float32)
    for c in range(8):
        b, j = divmod(c, 4)
        o = np.asarray(r[c]["outT"], np.float32).transpose(2, 1, 0).reshape(16, 128, D)
        out[b, j::4] = o
    return out.reshape(2, 8192, D)
```
